# Optimizing a Trainium2 kernel written in Bass

```python
import jax, jax.numpy as jnp
from jax import lax
import numpy as np

D_MODEL = 1024
BATCH = 16
SEQ = 2048
DEPTH = 1

HEAD_DIM = 64
NA_HEADS = 8
NA_WIDTH = NA_HEADS * HEAD_DIM
GRID_W = 64
NA_ROWS_MAX = 8
NA_COLS = 16
SW_HEADS = 8
SW_KV_HEADS = 2
SW_GROUP = SW_HEADS // SW_KV_HEADS
SW_WIDTH = SW_HEADS * HEAD_DIM
SW_KV_WIDTH = SW_KV_HEADS * HEAD_DIM
SW_WINDOW = 128
SW_BLOCK = 128
MIX_WIDTH = NA_WIDTH + SW_WIDTH
IN_WIDTH = 3 * NA_WIDTH + SW_WIDTH + 2 * SW_KV_WIDTH
D_FF = 2816
CONV_W = 3
ROPE_THETA = 10000.0
EPS = 1e-6
NEG = -1e30

kernel_name = "hybrid_natten_swa_convffn_block"


def rmsnorm(x, g):
    xf = x.astype(jnp.float32)
    y = xf * lax.rsqrt(jnp.mean(xf * xf, axis=-1, keepdims=True) + EPS)
    return (y * g.astype(jnp.float32)).astype(x.dtype)


def rope(t, pos):
    half = HEAD_DIM // 2
    inv = ROPE_THETA ** (-jnp.arange(half, dtype=jnp.float32) / half)
    ang = pos.astype(jnp.float32)[:, None] * inv[None, :]
    cos = jnp.cos(ang)[None, :, None, :].astype(t.dtype)
    sin = jnp.sin(ang)[None, :, None, :].astype(t.dtype)
    t1, t2 = t[..., :half], t[..., half:]
    return jnp.concatenate([t1 * cos - t2 * sin, t2 * cos + t1 * sin], axis=-1)


def neighbourhood_attention(q, k, v, rpb):
    B, S = q.shape[0], q.shape[1]
    rows = S // GRID_W
    wr = min(NA_ROWS_MAX, rows)

    def grid(t):
        return t.reshape(B, rows, GRID_W, NA_HEADS, HEAD_DIM).transpose(0, 3, 1, 2, 4)

    r = jnp.arange(rows)
    rs = jnp.clip(r - wr // 2, 0, rows - wr)
    row_idx = rs[:, None] + jnp.arange(wr)[None, :]
    kb = jnp.take(grid(k), row_idx, axis=2).reshape(B, NA_HEADS, rows, wr * GRID_W, HEAD_DIM)
    vb = jnp.take(grid(v), row_idx, axis=2).reshape(B, NA_HEADS, rows, wr * GRID_W, HEAD_DIM)

    col = jnp.arange(GRID_W)
    cs = jnp.clip(col - NA_COLS // 2, 0, GRID_W - NA_COLS)
    col_ok = (col[None, :] >= cs[:, None]) & (col[None, :] < cs[:, None] + NA_COLS)
    dc = jnp.clip(col[None, :] - col[:, None] + NA_COLS - 1, 0, 2 * NA_COLS - 2)
    dr = row_idx - r[:, None] + NA_ROWS_MAX - 1
    bias = rpb[:, dr[:, None, :, None], dc[None, :, None, :]]
    bias = jnp.where(col_ok[None, None, :, None, :], bias.astype(jnp.float32), NEG)
    bias = bias.reshape(NA_HEADS, rows, GRID_W, wr * GRID_W)

    s = jnp.einsum('bhrqd,bhrkd->bhrqk', grid(q), kb,
                   preferred_element_type=jnp.float32) * (HEAD_DIM ** -0.5) + bias[None]
    p = jax.nn.softmax(s, axis=-1).astype(v.dtype)
    o = jnp.einsum('bhrqk,bhrkd->bhrqd', p, vb)
    return o.transpose(0, 2, 3, 1, 4).reshape(B, S, NA_WIDTH)


def window_sink_attention(q, k, v, sink):
    B, S = q.shape[0], q.shape[1]
    nb = S // SW_BLOCK
    qb = q.reshape(B, nb, SW_BLOCK, SW_KV_HEADS, SW_GROUP, HEAD_DIM)

    def band(t):
        tp = jnp.pad(t, ((0, 0), (SW_BLOCK, SW_BLOCK), (0, 0), (0, 0)))
        tp = tp.reshape(B, nb + 2, SW_BLOCK, SW_KV_HEADS, HEAD_DIM)
        return jnp.concatenate([tp[:, :-2], tp[:, 1:-1], tp[:, 2:]], axis=2)

    kw, vw = band(k), band(v)
    blk = jnp.arange(nb)[:, None] * SW_BLOCK
    qpos = blk + jnp.arange(SW_BLOCK)[None, :]
    kpos = blk - SW_BLOCK + jnp.arange(3 * SW_BLOCK)[None, :]
    ok = (jnp.abs(qpos[:, :, None] - kpos[:, None, :]) <= SW_WINDOW) \
        & ((kpos >= 0) & (kpos < S))[:, None, :]

    s = jnp.einsum('bnqhgd,bnkhd->bhgnqk', qb, kw,
                   preferred_element_type=jnp.float32) * (HEAD_DIM ** -0.5)
    s = jnp.where(ok[None, None, None], s, NEG)
    sk = sink.astype(jnp.float32).reshape(1, SW_KV_HEADS, SW_GROUP, 1, 1)
    m = jnp.maximum(jnp.max(s, axis=-1), sk)
    p = jnp.exp(s - m[..., None])
    den = jnp.sum(p, axis=-1) + jnp.exp(sk - m)
    p = (p / den[..., None]).astype(v.dtype)
    o = jnp.einsum('bhgnqk,bnkhd->bnqhgd', p, vw)
    return o.reshape(B, S, SW_WIDTH)


def setup_inputs(seed: int = 0) -> dict:
    key = jax.random.key(seed)
    ks = jax.random.split(key, 20)
    L, D = DEPTH, D_MODEL

    def nrm(k, shape, scale):
        return jax.random.normal(k, shape, jnp.float32) * scale

    return {
        "x": nrm(ks[0], (BATCH, SEQ, D), 1.0),
        "c": nrm(ks[1], (BATCH, D), 1.0),
        "w_ada": nrm(ks[2], (L, D, 6 * D), 0.5 * D ** -0.5),
        "b_ada": nrm(ks[3], (L, 6 * D), 0.02),
        "g_attn": 1.0 + nrm(ks[4], (L, D), 0.02),
        "w_in": nrm(ks[5], (L, D, IN_WIDTH), D ** -0.5),
        "na_rpb": nrm(ks[6], (L, NA_HEADS, 2 * NA_ROWS_MAX - 1, 2 * NA_COLS - 1), 0.1),
        "sw_sink": nrm(ks[7], (L, SW_HEADS), 0.5),
        "g_na_out": 1.0 + nrm(ks[8], (L, NA_WIDTH), 0.02),
        "g_sw_out": 1.0 + nrm(ks[9], (L, SW_WIDTH), 0.02),
        "w_out": nrm(ks[10], (L, MIX_WIDTH, D), MIX_WIDTH ** -0.5),
        "g_ffn": 1.0 + nrm(ks[11], (L, D), 0.02),
        "w_up": nrm(ks[12], (L, D, 2 * D_FF), D ** -0.5),
        "conv_w": nrm(ks[13], (L, CONV_W, D_FF), CONV_W ** -0.5),
        "conv_b": nrm(ks[14], (L, D_FF), 0.02),
        "w_down": nrm(ks[15], (L, D_FF, D), D_FF ** -0.5),
        "g_final": 1.0 + nrm(ks[16], (D,), 0.02),
    }


def reference(x, c, w_ada, b_ada, g_attn, w_in, na_rpb, sw_sink, g_na_out, g_sw_out,
              w_out, g_ffn, w_up, conv_w, conv_b, w_down, g_final):
    S = x.shape[1]
    pos = jnp.arange(S)
    splits = [NA_WIDTH, 2 * NA_WIDTH, 3 * NA_WIDTH, 3 * NA_WIDTH + SW_WIDTH,
              3 * NA_WIDTH + SW_WIDTH + SW_KV_WIDTH]
    for l in range(DEPTH):
        mod = jax.nn.silu(c) @ w_ada[l] + b_ada[l]
        shift_a, scale_a, gate_a, shift_f, scale_f, gate_f = [
            m[:, None, :] for m in jnp.split(mod, 6, axis=-1)]

        h = rmsnorm(x, g_attn[l]) * (1.0 + scale_a) + shift_a
        proj = h @ w_in[l]
        qa, ka, va, qb, kb, vb = jnp.split(proj, splits, axis=-1)
        Bn = x.shape[0]
        qa = qa.reshape(Bn, S, NA_HEADS, HEAD_DIM)
        ka = ka.reshape(Bn, S, NA_HEADS, HEAD_DIM)
        va = va.reshape(Bn, S, NA_HEADS, HEAD_DIM)
        qb = rope(qb.reshape(Bn, S, SW_HEADS, HEAD_DIM), pos)
        kb = rope(kb.reshape(Bn, S, SW_KV_HEADS, HEAD_DIM), pos)
        vb = vb.reshape(Bn, S, SW_KV_HEADS, HEAD_DIM)

        o_a = rmsnorm(neighbourhood_attention(qa, ka, va, na_rpb[l]), g_na_out[l])
        o_b = rmsnorm(window_sink_attention(qb, kb, vb, sw_sink[l]), g_sw_out[l])
        mix = jnp.concatenate([o_a, o_b], axis=-1) @ w_out[l]
        x = x + gate_a * mix

        h = rmsnorm(x, g_ffn[l]) * (1.0 + scale_f) + shift_f
        val, gt = jnp.split(h @ w_up[l], 2, axis=-1)
        gp = jnp.pad(gt, ((0, 0), (1, 1), (0, 0)))
        cw = conv_w[l]
        gc = gp[:, :-2] * cw[0] + gp[:, 1:-1] * cw[1] + gp[:, 2:] * cw[2] + conv_b[l]
        x = x + gate_f * ((jax.nn.silu(gc) * val) @ w_down[l])
    return rmsnorm(x, g_final)
```

```python
import numpy as np
from contextlib import ExitStack

import concourse.bass as bass
import concourse.mybir as mybir
from concourse.bass_utils import run_bass_kernel_spmd

F32 = mybir.dt.float32
BF16 = mybir.dt.bfloat16
AF = mybir.ActivationFunctionType
ALU = mybir.AluOpType

D = 1024
SEQ = 2048
NT = 16
DFF = 2816
NCH = 22
EPS = 1e-6
NEG = -30000.0
N_CORES = 8
SEQ_PER_CORE = 2
GROUPS = [(0, 6), (6, 12), (12, 17), (17, 22)]


class Res:
    __slots__ = ("name", "w", "rd", "rdma")

    def __init__(self, name):
        self.name = name
        self.w = None
        self.rd = {}
        self.rdma = []


class Op:
    __slots__ = ("eng", "fn", "deps", "dma", "sem", "semval", "signals", "sigidx", "name")


ENGS = ["pe", "act", "dve", "pool", "sp"]


class Prog:
    def __init__(self, n_dma_sems):
        self.ops = []
        self.n_dma_sems = n_dma_sems
        self.dma_last = {q: [None] * n for q, n in n_dma_sems.items()}
        self.dma_cnt = {q: [0] * n for q, n in n_dma_sems.items()}
        self.dma_rr = {q: 0 for q in n_dma_sems}
        self.last_op = {e: None for e in ENGS}
        self.pending_dma = []

    def add(self, eng, fn, reads=(), writes=(), dma=False, name="", extra_deps=()):
        op = Op()
        op.eng, op.fn, op.dma, op.name = eng, fn, dma, name
        op.signals = False
        op.sigidx = 0
        op.sem = None
        op.semval = 0
        deps = list(extra_deps)
        for r in reads:
            if r.w is not None:
                deps.append(r.w)
        for w in writes:
            if w.w is not None:
                deps.append(w.w)
            deps.extend(w.rd.values())
            deps.extend(w.rdma)
        if dma:
            q = eng
            n = self.n_dma_sems[q]
            j = self.dma_rr[q]
            self.dma_rr[q] = (j + 1) % n
            prev = self.dma_last[q][j]
            if prev is not None:
                deps.append(prev)
            self.dma_cnt[q][j] += 1
            op.sem = (q, j)
            op.semval = 16 * self.dma_cnt[q][j]
            self.dma_last[q][j] = op
            self.pending_dma.append(op)
        for r in reads:
            if dma:
                r.rdma.append(op)
            else:
                r.rd[eng] = op
        for w in writes:
            w.w = op
            w.rd = {}
            w.rdma = []
        seen = set()
        dl = []
        for d in deps:
            if d is op or d is None or id(d) in seen:
                continue
            seen.add(id(d))
            dl.append(d)
        op.deps = dl
        self.ops.append(op)
        self.last_op[eng] = op
        return op

    def barrier(self):
        lasts = dict(self.last_op)
        pend = list(self.pending_dma)
        self.pending_dma = []
        for e in ENGS:
            deps = [o for ee, o in lasts.items() if ee != e and o is not None] + pend
            self.add(e, None, extra_deps=deps, name="barrier")

    def emit(self, nc, block, eng_sems, dma_sems):
        for op in self.ops:
            for d in op.deps:
                if not d.dma and (d.eng != op.eng or op.dma or d.eng != "pe"):
                    d.signals = True
        cnt = {e: 0 for e in ENGS}
        for op in self.ops:
            if (not op.dma) and op.signals:
                if op.fn is None:
                    op.signals = False
                    op.sigidx = cnt[op.eng]
                else:
                    cnt[op.eng] += 1
                    op.sigidx = cnt[op.eng]
        by_eng = {e: [o for o in self.ops if o.eng == e] for e in ENGS}

        def run(e, eng):
            waited = {}
            for op in by_eng[e]:
                need = {}
                for d in op.deps:
                    if d.dma:
                        key = ("dma",) + d.sem
                        val = d.semval
                    else:
                        if d.eng == op.eng and d.eng == "pe" and not op.dma:
                            continue
                        key = ("eng", d.eng)
                        val = d.sigidx
                    if val > need.get(key, 0):
                        need[key] = val
                for key, val in need.items():
                    if waited.get(key, 0) < val:
                        sem = eng_sems[key[1]] if key[0] == "eng" else dma_sems[key[1]][key[2]]
                        eng.wait_ge(sem, val)
                        waited[key] = val
                if op.fn is None:
                    continue
                ins = op.fn(eng)
                if op.dma:
                    ins.then_inc(dma_sems[op.sem[0]][op.sem[1]], 16)
                elif op.signals:
                    ins.then_inc(eng_sems[e], 1)

        @block.tensor
        def _(eng):
            run("pe", eng)

        @block.scalar
        def _(eng):
            run("act", eng)

        @block.vector
        def _(eng):
            run("dve", eng)

        @block.gpsimd
        def _(eng):
            run("pool", eng)

        @block.sync
        def _(eng):
            run("sp", eng)


def na_tiles(j):
    lo = min(max(j - 2, 0), 12)
    hi = max(min(j + 2, 15), 3)
    return lo, hi


def na_table_struct():
    sigs = {}
    tbl_of = {}
    for j in range(16):
        lo, hi = na_tiles(j)
        for i in range(lo, hi + 1):
            sig = []
            for qr in (2 * j, 2 * j + 1):
                rs = min(max(qr - 4, 0), 24)
                for kr in (2 * i, 2 * i + 1):
                    sig.append((kr - qr) if (rs <= kr < rs + 8) else None)
            sig = tuple(sig)
            if sig not in sigs:
                sigs[sig] = len(sigs)
            tbl_of[(j, i)] = sigs[sig]
    return tbl_of, sigs


NA_TBL_OF, NA_SIGS = na_table_struct()
N_NATBL = len(NA_SIGS)


def build_na_bias(rpb):
    kc = np.arange(64)[:, None]
    qc = np.arange(64)[None, :]
    cs = np.clip(qc - 8, 0, 48)
    colok = (kc >= cs) & (kc < cs + 16)
    dc = np.clip(kc - qc + 15, 0, 30)
    out = np.full((128, 8, N_NATBL, 128), NEG, dtype=np.float32)
    for sig, t in NA_SIGS.items():
        for qri in range(2):
            for kri in range(2):
                off = sig[qri * 2 + kri]
                if off is None:
                    continue
                for h in range(8):
                    blk = np.where(colok, rpb[h, off + 7][dc], np.float32(NEG))
                    out[kri * 64:(kri + 1) * 64, h, t, qri * 64:(qri + 1) * 64] = blk
    return np.ascontiguousarray(out.reshape(128, 8 * N_NATBL, 128))


def build_sw_mask():
    kk = np.arange(128)[:, None]
    qq = np.arange(128)[None, :]
    m = np.zeros((128, 3, 128), dtype=np.float32)
    m[:, 0, :] = np.where(kk >= qq, 0.0, NEG)
    m[:, 1, :] = np.where(kk <= qq, 0.0, NEG)
    return m


def build_rope():
    half = 32
    inv = (np.float32(10000.0) ** (-(np.arange(half, dtype=np.float32) / np.float32(half)))).astype(np.float32)
    pos = np.arange(SEQ, dtype=np.float32)
    ang = (pos[:, None] * inv[None, :]).astype(np.float32)
    cos = np.cos(ang).astype(np.float32).T
    sin = np.sin(ang).astype(np.float32).T
    C = np.zeros((128, SEQ), np.float32)
    S = np.zeros((128, SEQ), np.float32)
    for p in range(128):
        d = p % 64
        i = d % 32
        C[p] = cos[i]
        S[p] = -sin[i] if d < 32 else sin[i]
    return C, S


def build_perm():
    pm = np.zeros((128, 128), np.float32)
    for m in range(128):
        hb, d = divmod(m, 64)
        pm[hb * 64 + (d + 32) % 64, m] = 1.0
    return pm


class _Stop(Exception):
    pass


def build_program(stop_after=None, dumps=()):
    nc = bass.Bass("TRN2", target_bir_lowering=False)
    P = Prog({"sp": 12, "pool": 10})
    dump_src = {}

    def ckpt(tag):
        if stop_after != tag:
            return
        P.barrier()
        for nm in dumps:
            src = dump_src[nm]
            dt_ = nc.dram_tensor("dbg_" + nm, list(src.shape), src.dtype, kind="ExternalOutput").ap()
            P.add("sp", lambda e, o=dt_, i=src: e.dma_start(out=o, in_=i), [], [], dma=True, name="dump")
        raise _Stop()

    def din(name, shape):
        return nc.dram_tensor(name, list(shape), F32, kind="ExternalInput").ap()

    x_d = din("x", (SEQ_PER_CORE, SEQ, D))
    wada_d = din("w_ada", (D, 6 * D))
    win_d = din("w_in", (D, 2304))
    wout_d = din("w_out", (D, D))
    wup_d = din("w_up", (D, 2 * DFF))
    wdn_d = din("w_down", (DFF, D))
    params_d = din("params", (256, 128))
    gfin_d = din("g_final", (D,))
    sink_d = din("sw_sink", (8,))
    nab_d = din("nabias", (128, 8 * N_NATBL, 128))
    swm_d = din("swmask", (128, 3, 128))
    ropec_d = din("rope_c", (128, SEQ))
    ropes_d = din("rope_s", (128, SEQ))
    ident_d = din("ident", (128, 128))
    perm_d = din("perm", (128, 128))
    out_d = nc.dram_tensor("out", [SEQ_PER_CORE, SEQ, D], F32, kind="ExternalOutput").ap()

    cur = [16512]
    LIMIT = 229344

    def alloc(name, shape, dt, at=None):
        nbytes = int(np.prod(shape[1:])) * (4 if dt == F32 else 2)
        nbytes = (nbytes + 31) // 32 * 32
        if at is None:
            off = cur[0]
            cur[0] += nbytes
        else:
            off = at
        assert off + nbytes <= LIMIT, (name, off, nbytes)
        return nc.alloc_sbuf_tensor_at(name, list(shape), dt, offset=off)

    ident_bf = alloc("ident_bf", [128, 128], BF16)
    ident_f = alloc("ident_f", [128, 128], F32)
    perm_f = alloc("perm_f", [128, 128], F32)
    paramsT = alloc("paramsT", [128, 256], F32)
    modT = alloc("modT", [128, 48, 2], F32)
    modv = alloc("modv", [128, 2, 6, 8], F32)
    esink = alloc("esink", [128, 8], F32)
    swm = alloc("swm", [128, 3, 128], BF16)
    scT = alloc("scT", [128, 2, 8], BF16)
    pstage = [alloc(f"pstage{i}", [128, 128], F32) for i in range(2)]
    tmpf = [alloc(f"tmpf{i}", [128, 128], F32) for i in range(2)]
    NSTAT = 8
    stat = alloc("stat", [128, NSTAT, 4], F32)
    rden_t = alloc("rden_t", [128, 2, 8], F32)
    dtmp_t = alloc("dtmp_t", [128, 2, 8], F32)
    tmp8 = alloc("tmp8", [128, 8], F32)
    tmp8b = alloc("tmp8b", [128, 8], F32)
    epsc = alloc("epsc", [128, 8], F32)
    A0 = cur[0]
    ARENA = LIMIT - A0

    def at(off):
        return A0 + off

    R0, R1, R2, R3 = 0, 32768, 114944, 135488
    hT = alloc("hT", [128, 8, SEQ], BF16, at(R0))
    qna = alloc("qna", [128, 4, SEQ], BF16, at(R1))
    knam = alloc("knam", [128, 8, SEQ], BF16, at(R1 + 16384))
    vna = alloc("vna", [128, NT, 8, 65], BF16, at(R1 + 49152))
    qtsw = alloc("qtsw", [128, 4, SEQ], BF16, at(R1 + 49152 + 16640))
    ktswm = alloc("ktswm", [128, 4, SEQ], BF16, at(R2))
    vsw = alloc("vsw", [128, NT, 2, 65], BF16, at(R2 + 16384))
    win_sb = alloc("win_sb", [128, 8, 2432], BF16, at(R3))
    ropec = alloc("ropec", [128, SEQ], F32, at(R3 + 38912))
    ropes = alloc("ropes", [128, SEQ], F32, at(R3 + 38912 + 8192))
    T3 = R3 + 38912 + 16384
    xin = [alloc(f"xin{i}", [128, 1024], F32, at(T3 + 4096 * i)) for i in range(2)]
    xn = [alloc(f"xn{i}", [128, 1024], BF16, at(T3 + 8192 + 2048 * i)) for i in range(2)]
    qf = [alloc(f"qf{i}", [128, 512], F32, at(T3 + 2048 * i)) for i in range(2)]
    ra = [alloc(f"ra{i}", [128, 512], F32, at(T3 + 4096 + 2048 * i)) for i in range(2)]
    wada_sb = [alloc(f"wada{i}", [128, 8, 512], BF16, at(R3 + 8192 * i)) for i in range(2)]
    nab = alloc("nab", [128, 8 * N_NATBL, 128], BF16, at(R0))
    cat = alloc("cat", [128, NT, 1024], BF16, at(R3))
    ptna = [alloc(f"ptna{i}", [128, 640], BF16, at(R3 + 32768 + 1280 * i)) for i in range(3)]
    ptsw = [alloc(f"ptsw{i}", [128, 3, 512], BF16, at(R3 + 32768 + 3072 * i)) for i in range(2)]
    osb = [alloc(f"osb{i}", [128, 512], F32, at(R3 + 32768 + 6144 + 2048 * i)) for i in range(2)]
    woutg = alloc("woutg", [128, 8, 1024], BF16, at(R3 + 45056))
    wstg = [alloc(f"wstg{i}", [128, 1024], F32, at(R3 + 61440 + 4096 * i)) for i in range(2)]
    cT = [alloc(f"cT{i}", [128, 8, 128], BF16, at(R3 + 32768 + 2048 * i)) for i in range(2)]
    xin5 = [alloc(f"xin5_{i}", [128, 1024], F32, at(R3 + 36864 + 4096 * i)) for i in range(2)]
    xn5 = [alloc(f"xn5_{i}", [128, 1024], BF16, at(R1 + 65536 + 12288 + 2048 * i)) for i in range(2)]
    gatea_bc = alloc("gatea_bc", [128, 1024], F32, at(R0 + 18432))
    gatef_bc = alloc("gatef_bc", [128, 1024], F32, at(R1 + 65536 + 4096))
    gfin_bc = alloc("gfin_bc", [128, 1024], F32, at(R1 + 65536 + 8192))
    x1 = alloc("x1", [128, NT, 1024], F32, at(R1))
    h2T = alloc("h2T", [128, 8, SEQ], BF16, at(R0))
    wdng = alloc("wdng", [128, 6, 1024], BF16, at(R2))
    uT = alloc("uT", [128, 6, SEQ], BF16, at(R3))
    Gb = [alloc(f"Gb{i}", [128, 2052], F32, at(R3 + 24576 + 8224 * i)) for i in range(2)]
    T1 = [alloc(f"T1_{i}", [128, SEQ], F32, at(R3 + 41024 + 8192 * i)) for i in range(2)]
    wup_sb = [alloc(f"wup{i}", [128, 8, 256], BF16, at(R3 + 57408 + 4096 * i)) for i in range(3)]
    wdstg = [alloc(f"wdstg{i}", [128, 1024], F32, at(R2 + 12288 + 4096 * i)) for i in range(2)]
    ostg = [alloc("ostg0", [128, 1024], F32, at(R1 + 65536 + 12288)), alloc("ostg1", [128, 1024], F32, at(R1 + 65536))]

    ps = nc.alloc_psum_tensor("ps", [128, 4096], F32)

    def bank(b, n=1):
        return ps[:, 512 * b:512 * (b + n)]

    def bank_bf(b):
        return ps[:, 512 * b:512 * (b + 1)].bitcast(BF16)

    pb = [Res(f"psum{b}") for b in range(8)]

    class RS:
        pass

    r = RS()
    r.const = Res("const")
    r.params = Res("params")
    r.mod = Res("mod")
    r.gatea = Res("gatea")
    r.gatef = Res("gatef")
    r.stat = [Res(f"stat{i}") for i in range(NSTAT)]
    r.rden = [Res("rden0"), Res("rden1")]
    r.dtmp = [Res("dtmp0"), Res("dtmp1")]
    r.tmp8 = Res("tmp8")
    r.tmp8b = Res("tmp8b")
    r.modv = [[Res(f"modv{b}_{k}") for k in range(6)] for b in range(2)]
    r.hT2 = [[Res(f"hT{t}a"), Res(f"hT{t}b")] for t in range(NT)]
    r.hT = [x for p in r.hT2 for x in p]
    r.qna = [[Res(f"qna{c}_{b}") for b in range(4)] for c in range(4)]
    r.knam = [[Res(f"knam{h}_{b}") for b in range(4)] for h in range(8)]
    r.gfin = Res("gfin")
    r.vna = [Res(f"vna{t}") for t in range(NT)]
    r.qtsw = [[Res(f"qtsw{c}_{b}") for b in range(4)] for c in range(4)]
    r.ktswm = [[Res(f"ktswm{c}_{b}") for b in range(4)] for c in range(4)]
    r.vsw = [Res(f"vsw{t}") for t in range(NT)]
    r.winb = {k: Res("win_" + k) for k in ("qa", "ka", "va", "vb", "qb", "kb0", "kb1", "kb2", "kb3")}
    r.ropeC = Res("ropeC")
    r.ropeS = Res("ropeS")
    r.xin = [Res("xin0"), Res("xin1")]
    r.xn = [Res("xn0"), Res("xn1")]
    r.qf = [Res("qf0"), Res("qf1")]
    r.ra = [Res("ra0"), Res("ra1")]
    r.wada = [Res("wada0"), Res("wada1")]
    r.pstage = [Res("pstage0"), Res("pstage1")]
    r.tmpf = [Res("tmpf0"), Res("tmpf1")]
    r.nabh = [Res(f"nab{h}") for h in range(8)]
    r.cat = [[Res(f"cat{t}_{h}") for h in range(2)] for t in range(NT)]
    r.ptna = [Res(f"ptna{i}") for i in range(3)]
    r.ptsw = [Res(f"ptsw{i}") for i in range(2)]
    r.osb = [Res("osb0"), Res("osb1")]
    r.woutg = Res("woutg")
    r.wstg = [Res("wstg0"), Res("wstg1")]
    r.cT = [Res("cT0"), Res("cT1")]
    r.x1 = [Res(f"x1_{t}") for t in range(NT)]
    r.h2T2 = [[Res(f"h2T{t}a"), Res(f"h2T{t}b")] for t in range(NT)]
    r.h2T = [x for p in r.h2T2 for x in p]
    r.wdng = [Res(f"wdng{i}") for i in range(6)]
    r.uT = [[Res(f"uT{c}_{b}") for b in range(4)] for c in range(6)]
    r.G = [Res("G0"), Res("G1")]
    r.T1 = [Res("T1_0"), Res("T1_1")]
    r.wup = [Res("wup0"), Res("wup1"), Res("wup2")]
    r.wupg = [Res("wupg0"), Res("wupg1"), Res("wupg2")]
    r.wdstg = [Res("wdstg0"), Res("wdstg1")]
    r.ostg = [Res("ostg0"), Res("ostg1")]

    statc = [0]

    def new_stat():
        i = statc[0] % NSTAT
        statc[0] += 1
        return stat[:, i, :], r.stat[i]

    def dma(q, out, in_, reads=(), writes=(), name=""):
        return P.add(q, lambda e, o=out, i=in_: e.dma_start(out=o, in_=i), reads, writes, dma=True, name=name)

    def act(fn, reads, writes, name=""):
        return P.add("act", fn, reads, writes, name=name)

    def dve(fn, reads, writes, name=""):
        return P.add("dve", fn, reads, writes, name=name)

    def pool(fn, reads, writes, name=""):
        return P.add("pool", fn, reads, writes, name=name)

    def pe(fn, reads, writes, name=""):
        return P.add("pe", fn, reads, writes, name=name)

    def rstd_from_ssq(st, sres, n):
        act(lambda e, st=st: e.activation(out=st[:, 2:3], in_=st[:, 0:1], func=AF.Ln, scale=1.0 / n, bias=epsc[:, 0:1]),
            [sres, r.const], [sres], "rstd_ln")
        act(lambda e, st=st: e.activation(out=st[:, 1:2], in_=st[:, 2:3], func=AF.Exp, scale=-0.5),
            [sres], [sres], "rstd_exp")

    def rms_stats(src_ap, src_res, xnbuf, xnres):
        st, sres = new_stat()

        def f1(e, st=st):
            return e.activation(out=xnbuf[:, :], in_=src_ap, func=AF.Square, accum_out=st[:, 0:1])
        act(f1, [src_res], [sres, xnres], "ssq")
        rstd_from_ssq(st, sres, 1024.0)
        return st, sres

    def rms_apply(src_ap, src_res, st, sres, xnbuf, xnres, pstb, dstT, dst_res, t, Acol, Bcol, mres, evac_dve=False,
                  part="all"):
        if part in ("all", "xn"):
            dve(lambda e, st=st: e.tensor_scalar(xnbuf[:, :], src_ap, st[:, 1:2], None, ALU.mult),
                [src_res, sres], [xnres], "xn")
        if part == "xn":
            return
        pbf = bank_bf(pstb)

        def ft(e):
            ins = None
            for c in range(8):
                ins = e.transpose(pbf[:, c * 128:(c + 1) * 128], xnbuf[:, c * 128:(c + 1) * 128], ident_bf[:, :])
            return ins
        pe(ft, [xnres, r.const], [pb[pstb]], "xnT")

        def fe(e):
            ins = None
            for c in range(8):
                ins = e.activation(out=dstT[:, c, t * 128:(t + 1) * 128], in_=pbf[:, c * 128:(c + 1) * 128],
                                   func=AF.Identity, bias=Bcol(c), scale=Acol(c))
            return ins
        if evac_dve:
            def fe2(e):
                ins = None
                for c in range(8):
                    ins = e.tensor_scalar(dstT[:, c, t * 128:(t + 1) * 128], pbf[:, c * 128:(c + 1) * 128],
                                          Acol(c), Bcol(c), ALU.mult, ALU.add)
                return ins
            dve(fe2, [pb[pstb]] + list(mres), list(dst_res), "hT")
        else:
            act(fe, [pb[pstb]] + list(mres), list(dst_res), "hT")

    dump_src.update(dict(paramsT=paramsT[:, :], modT=modT[:, :, :], modv=modv[:, :, :, :], hT=hT[:, :, :],
                         gatea=gatea_bc[:, :], gatef=gatef_bc[:, :], qna=qna[:, :, :], knam=knam[:, :, :], vna=vna[:, :, :, :],
                         qtsw=qtsw[:, :, :], ktswm=ktswm[:, :, :], vsw=vsw[:, :, :, :], cat=cat[:, :, :],
                         x1=x1[:, :, :], h2T=h2T[:, :, :], esink=esink[:, :], scT=scT[:, :, :]))
    try:
        dve(lambda e: e.memset(epsc[:, :], EPS), [], [r.const], "eps")
        dma("sp", ident_f[:, :], ident_d, [], [r.const])
        dma("sp", perm_f[:, :], perm_d, [], [r.const])
        dma("pool", ident_bf[:, :], ident_d, [], [r.const])
        dma("pool", swm[:, :, :], swm_d, [], [r.const])
        dma("sp", esink[:, :], sink_d.partition_broadcast(128), [], [r.const])
        act(lambda e: e.activation(out=esink[:, :], in_=esink[:, :], func=AF.Exp), [r.const], [r.const], "esink")
        for u in range(2):
            dma("sp", pstage[u][:, :], params_d[u * 128:(u + 1) * 128, :], [], [r.pstage[u]])
            pe(lambda e, u=u: e.transpose(bank(1 + u)[:, 0:128], pstage[u][:, :], ident_f[:, :]),
               [r.pstage[u], r.const], [pb[1 + u]], "paramsT")
            dve(lambda e, u=u: e.tensor_copy(paramsT[:, u * 128:(u + 1) * 128], bank(1 + u)[:, 0:128]),
                [pb[1 + u]], [r.params])
        act(lambda e: e.activation(out=scT[:, :, :].rearrange("p b k -> p (b k)"), in_=paramsT[:, 160:176], func=AF.Silu),
            [r.params], [r.mod], "silu_c")
        wada_v = wada_d.rearrange("(k p) f -> p k f", p=128)
        for ct in range(12):
            wb = ct % 2
            dma("pool", wada_sb[wb][:, :, :], wada_v[:, :, ct * 512:(ct + 1) * 512], [], [r.wada[wb]])

            def fm(e, ct=ct, wb=wb):
                ins = None
                for fcl in range(4):
                    fc = ct * 4 + fcl
                    for k in range(8):
                        ins = e.matmul(bank(0)[:, fc * 2:fc * 2 + 2], wada_sb[wb][:, k, fcl * 128:(fcl + 1) * 128],
                                       scT[:, :, k], start=(k == 0), stop=(k == 7))
                return ins
            pe(fm, [r.wada[wb], r.mod], [pb[0]], "modmm")
        dve(lambda e: e.tensor_tensor(modT[:, :, :], bank(0)[:, 0:96].rearrange("p (f b) -> p f b", b=2),
                                      paramsT[:, 0:48].unsqueeze(2).broadcast_to([128, 48, 2]), ALU.add),
            [pb[0], r.params], [r.mod], "modT")
        for b in range(2):
            ta, tb_ = tmp8[:, 0:8], tmp8b[:, 0:8]
            dve(lambda e, b=b, ta=ta: e.tensor_scalar(ta, modT[:, 8:16, b], 1.0, None, ALU.add), [r.mod], [r.tmp8], "m1")
            dve(lambda e, b=b, ta=ta: e.tensor_tensor(modv[:, b, 0, :], ta, paramsT[:, 48:56], ALU.mult),
                [r.tmp8, r.params], [r.modv[b][0]], "m2")
            dve(lambda e, b=b: e.tensor_copy(modv[:, b, 1, :], modT[:, 0:8, b]), [r.mod], [r.modv[b][1]], "m3")
            dve(lambda e, b=b: e.tensor_copy(modv[:, b, 2, :], modT[:, 16:24, b]), [r.mod], [r.modv[b][2]], "m4")
            dve(lambda e, b=b, tb_=tb_: e.tensor_scalar(tb_, modT[:, 32:40, b], 1.0, None, ALU.add), [r.mod], [r.tmp8b], "m5")
            dve(lambda e, b=b, tb_=tb_: e.tensor_tensor(modv[:, b, 3, :], tb_, paramsT[:, 56:64], ALU.mult),
                [r.tmp8b, r.params], [r.modv[b][3]], "m6")
            dve(lambda e, b=b: e.tensor_copy(modv[:, b, 4, :], modT[:, 24:32, b]), [r.mod], [r.modv[b][4]], "m7")
            dve(lambda e, b=b: e.tensor_copy(modv[:, b, 5, :], modT[:, 40:48, b]), [r.mod], [r.modv[b][5]], "m8")

        def build_gate_bc(b, kind, dst, dres):
            for c in range(8):
                tb = c % 2
                dve(lambda e, c=c, tb=tb: e.tensor_copy(tmpf[tb][:, :], modv[:, b, kind, c:c + 1].to_broadcast([128, 128])),
                    [r.modv[b][kind]], [r.tmpf[tb]], "gbc0")
                pe(lambda e, tb=tb: e.transpose(bank(1 + tb)[:, 0:128], tmpf[tb][:, :], ident_f[:, :]),
                   [r.tmpf[tb], r.const], [pb[1 + tb]], "gbcT")
                dve(lambda e, c=c, tb=tb: e.tensor_copy(dst[:, c * 128:(c + 1) * 128], bank(1 + tb)[:, 0:128]),
                    [pb[1 + tb]], [dres], "gbc1")

        ckpt("P0")
        out_dmas = []
        for s in range(SEQ_PER_CORE):
            if s > 0:
                P.barrier()
            win_v = win_d.rearrange("(k p) f -> p k f", p=128)
            dma("pool", win_sb[:, :, 1024:1536], win_v[:, :, 1024:1536], [], [r.winb["va"]] + r.wada)
            dma("pool", win_sb[:, :, 2304:2432], win_v[:, :, 2176:2304], [], [r.winb["vb"]] + r.wada)
            dma("pool", win_sb[:, :, 0:512], win_v[:, :, 0:512], [], [r.winb["qa"]] + r.wada)
            def fones(e):
                e.memset(vna[:, :, :, 64:65], 1.0)
                return e.memset(vsw[:, :, :, 64:65], 1.0)
            pool(fones, [], r.vna + r.vsw, "ones")

            def fzero(e):
                kv = knam[:, :, :].rearrange("p (a two) t -> p a two t", two=2)
                e.memset(kv[64:128, :, 0, :], 0.0)
                e.memset(kv[0:64, :, 1, :], 0.0)
                sv = ktswm[:, :, :].rearrange("p (a two) t -> p a two t", two=2)
                e.memset(sv[64:128, :, 0, :], 0.0)
                return e.memset(sv[0:64, :, 1, :], 0.0)
            pool(fzero, [], [x for l in r.knam for x in l] + [x for l in r.ktswm for x in l], "kzero")
            dma("pool", win_sb[:, :, 512:1024], win_v[:, :, 512:1024], [], [r.winb["ka"]] + r.wada)
            dma("pool", win_sb[:, :, 1536:2048], win_v[:, :, 1536:2048], [], [r.winb["qb"]] + r.wada)
            dma("pool", win_sb[:, :, 2048:2112], win_v[:, :, 2048:2112], [], [r.winb["kb0"]] + r.wada)
            dma("pool", win_sb[:, :, 2112:2176], win_v[:, :, 2048:2112], [], [r.winb["kb1"]] + r.wada)
            dma("pool", win_sb[:, :, 2176:2240], win_v[:, :, 2112:2176], [], [r.winb["kb2"]] + r.wada)
            dma("pool", win_sb[:, :, 2240:2304], win_v[:, :, 2112:2176], [], [r.winb["kb3"]] + r.wada)

            p1st = {}

            def p1_load(t):
                dma("sp", xin[t % 2][:, :], x_d[s, t * 128:(t + 1) * 128, :], [], [r.xin[t % 2]])

            def p1_iter(it):
                def ap(t, part):
                    xb = t % 2
                    st_, sres_ = p1st[t]
                    rms_apply(xin[xb][:, :], r.xin[xb], st_, sres_, xn[xb], r.xn[xb], 6 + xb, hT, r.hT2[t], t,
                              lambda c, s=s: modv[:, s, 0, c:c + 1], lambda c, s=s: modv[:, s, 1, c:c + 1], r.modv[s][0:2],
                              evac_dve=True, part=part)
                if it < NT:
                    t, xb = it, it % 2
                    p1st[t] = rms_stats(xin[xb][:, :], r.xin[xb], xn[xb], r.xn[xb])
                if it >= 1:
                    ap(it - 1, "rest")
                if it < NT:
                    ap(it, "xn")
                if it + 1 < NT:
                    p1_load(it + 1)

            fm_plain, fm_rope = [], []
            for cc in range(4):
                fm_plain.append(("qa", qna, cc, r.qna[cc], cc * 128))
            for cc in range(4):
                fm_plain.append(("ka", knam, cc, None, 512 + cc * 128))
            for cc in range(4):
                fm_rope.append(("qb", qtsw, cc, r.qtsw[cc], 1536 + cc * 128))
            for cc in range(2):
                fm_rope.append(("kb", ktswm, cc, None, 2048 + cc * 128))
            pcnt = [0]
            rcnt = [0]
            pending = []
            first_rope = [True]

            def p2_group(kind, dst, dc_, dres, col, b4):
                pbk = pcnt[0] % 3
                pcnt[0] += 1

                def fmm(e):
                    ins = None
                    for k in range(8):
                        ins = e.matmul(bank(pbk), win_sb[:, k, col:col + 128], hT[:, k, b4 * 512:(b4 + 1) * 512],
                                       start=(k == 0), stop=(k == 7))
                    return ins
                wres = [r.winb[kind]] if kind != "kb" else [r.winb["kb%d" % i] for i in range(4)]
                pe(fmm, wres + r.hT[8 * b4:8 * b4 + 8], [pb[pbk]], "proj")
                bsl = slice(b4 * 512, (b4 + 1) * 512)
                dsl = dst[:, dc_, bsl]
                if kind == "qa":
                    act(lambda e: e.activation(out=dsl, in_=bank(pbk), func=AF.Copy, scale=0.125),
                        [pb[pbk]], [dres[b4]], "qa")
                elif kind == "ka":
                    def fka(e):
                        e.activation(out=knam[0:64, 2 * dc_, bsl], in_=bank(pbk)[0:64, :], func=AF.Copy)
                        return e.activation(out=knam[64:128, 2 * dc_ + 1, bsl], in_=bank(pbk)[64:128, :], func=AF.Copy)
                    act(fka, [pb[pbk]], [r.knam[2 * dc_][b4], r.knam[2 * dc_ + 1][b4]], "ka")
                else:
                    rb = rcnt[0] % 2
                    rcnt[0] += 1
                    sc = 0.125 if kind == "qb" else 1.0
                    extra = [r.xin[0], r.xin[1], r.xn[0], r.xn[1]] if first_rope[0] else []
                    first_rope[0] = False
                    act(lambda e: e.activation(out=qf[rb][:, :], in_=bank(pbk), func=AF.Copy, scale=sc),
                        [pb[pbk]], [r.qf[rb]] + extra, "qf")

                    def post():
                        pe(lambda e: e.matmul(bank(3 + rb), perm_f[:, :], qf[rb][:, :], start=True, stop=True),
                           [r.qf[rb], r.const], [pb[3 + rb]], "perm")
                        pool(lambda e: e.tensor_tensor(ra[rb][:, :], qf[rb][:, :], ropec[:, bsl], ALU.mult),
                             [r.qf[rb], r.ropeC], [r.ra[rb]], "ropeA")
                        dve(lambda e: e.tensor_tensor(qf[rb][:, :], bank(3 + rb), ropes[:, bsl], ALU.mult),
                            [pb[3 + rb], r.ropeS], [r.qf[rb]], "ropeB")
                        if kind == "qb":
                            dve(lambda e: e.tensor_tensor(dsl, qf[rb][:, :], ra[rb][:, :], ALU.add),
                                [r.ra[rb], r.qf[rb]], [dres[b4]], "ropeC")
                        else:
                            def fkb(e):
                                e.tensor_tensor(ktswm[0:64, 2 * dc_, bsl], qf[rb][0:64, :], ra[rb][0:64, :], ALU.add)
                                return e.tensor_tensor(ktswm[64:128, 2 * dc_ + 1, bsl], qf[rb][64:128, :], ra[rb][64:128, :], ALU.add)
                            dve(fkb, [r.ra[rb], r.qf[rb]], [r.ktswm[2 * dc_][b4], r.ktswm[2 * dc_ + 1][b4]], "ropeC")
                    pending.append(post)
                    if len(pending) > 1:
                        pending.pop(0)()

            def p2_v(t):
                hv = (t % 2) * 128

                def fv(e):
                    ins = None
                    for k in range(8):
                        ins = e.matmul(bank(5), hT[:, k, t * 128:(t + 1) * 128], win_sb[:, k, 1024:1536],
                                       start=(k == 0), stop=(k == 7))
                    for k in range(8):
                        ins = e.matmul(bank(4)[:, hv:hv + 128], hT[:, k, t * 128:(t + 1) * 128],
                                       win_sb[:, k, 2304:2432], start=(k == 0), stop=(k == 7))
                    return ins
                pe(fv, [r.winb["va"], r.winb["vb"]] + r.hT2[t], [pb[5], pb[4]], "vproj")
                dve(lambda e: e.tensor_copy(vna[:, t, :, 0:64], bank(5).rearrange("p (h d) -> p h d", d=64)),
                    [pb[5]], [r.vna[t]], "vna")
                dve(lambda e: e.tensor_copy(vsw[:, t, :, 0:64], bank(4)[:, hv:hv + 128].rearrange("p (h d) -> p h d", d=64)),
                    [pb[4]], [r.vsw[t]], "vsw")

            p1_load(0)
            p1_iter(0)
            for it in range(1, 5):
                p1_iter(it)
                p2_v(it - 1)
            for b4 in range(4):
                if b4 == 1:
                    dma("sp", ropec[:, :], ropec_d, [], [r.ropeC])
                    dma("sp", ropes[:, :], ropes_d, [], [r.ropeS])
                slots = [lambda ch=ch, b4=b4: p2_group(*ch, b4) for ch in fm_plain] + \
                        ([lambda t=t: p2_v(t) for t in range(4 * b4, 4 * b4 + 4)] if b4 > 0 else [])
                nxt = [it for it in range(4 * b4 + 5, 4 * b4 + 9) if it <= NT]
                for si, slot in enumerate(slots):
                    slot()
                    if si % 3 == 2 and nxt:
                        p1_iter(nxt.pop(0))
                while nxt:
                    p1_iter(nxt.pop(0))
            ckpt("P1")
            for ch in fm_rope:
                for b4 in range(4):
                    p2_group(*ch, b4)
            while pending:
                pending.pop(0)()

            ckpt("P2")
            P.barrier()
            for h_ in range(8):
                dma("pool", nab[:, h_ * N_NATBL:(h_ + 1) * N_NATBL, :], nab_d[:, h_ * N_NATBL:(h_ + 1) * N_NATBL, :], [], [r.nabh[h_]])
            build_gate_bc(s, 2, gatea_bc, r.gatea)

            def issue_wout_fold(c):
                wb = c % 2
                dma("sp", wstg[wb][:, :], wout_d[c * 128:(c + 1) * 128, :], [], [r.wstg[wb]])
                dve(lambda e: e.scalar_tensor_tensor(woutg[:, c, :], wstg[wb][:, :], paramsT[:, 64 + c:65 + c],
                                                     gatea_bc[:, :], ALU.mult, ALU.mult),
                    [r.wstg[wb], r.params, r.gatea], [r.woutg], "woutg")

            def oacc_view(b0):
                return [ps[:, 512 * (b0 + k):512 * (b0 + k + 1)].rearrange("p (h d) -> p h d", d=128) for k in range(2)]

            def attn_finish(ob, tile, half, with_sink, cnt, defer_b=False):
                ov = oacc_view(ob)
                rd = cnt % 2
                rdv = [rden_t[:, rd, 4 * k:4 * k + 4] for k in range(2)]
                if with_sink:
                    dtv = [dtmp_t[:, rd, 4 * k:4 * k + 4] for k in range(2)]

                    def fs(e):
                        ins = None
                        for k in range(2):
                            ins = e.tensor_tensor(dtv[k], ov[k][:, :, 64], esink[:, 4 * k:4 * k + 4], ALU.add)
                        return ins
                    dve(fs, [pb[ob], pb[ob + 1], r.const], [r.dtmp[rd]], "dsink")

                    def fr(e):
                        ins = None
                        for k in range(2):
                            ins = e.reciprocal(rdv[k], dtv[k])
                        return ins
                    dve(fr, [r.dtmp[rd]], [r.rden[rd]], "rden")
                else:
                    def fr(e):
                        ins = None
                        for k in range(2):
                            ins = e.reciprocal(rdv[k], ov[k][:, :, 64])
                        return ins
                    dve(fr, [pb[ob], pb[ob + 1]], [r.rden[rd]], "rden")
                ob_ = osb[rd]

                def fn_(e):
                    ins = None
                    for k in range(2):
                        ins = e.tensor_tensor(ob_[:, 256 * k:256 * (k + 1)].rearrange("p (h d) -> p h d", h=4), ov[k][:, :, 0:64],
                                              rdv[k].unsqueeze(2).broadcast_to([128, 4, 64]), ALU.mult)
                    return ins
                dve(fn_, [pb[ob], pb[ob + 1], r.rden[rd]], [r.osb[rd]], "onorm")

                def part_b():
                    st, sres = new_stat()
                    dsl = cat[:, tile, half * 512:(half + 1) * 512]
                    act(lambda e: e.activation(out=dsl, in_=ob_[:, :], func=AF.Square, accum_out=st[:, 0:1]),
                        [r.osb[rd]], [sres, r.cat[tile][half]], "ossq")
                    rstd_from_ssq(st, sres, 512.0)
                    dve(lambda e: e.tensor_scalar(dsl, ob_[:, :], st[:, 1:2], None, ALU.mult),
                        [r.osb[rd], sres], [r.cat[tile][half]], "ocat")
                if defer_b:
                    return part_b
                part_b()
                return None

            items = [(j, h) for j in range(NT) for h in range(8)]
            import os as _os
            if _os.environ.get("DBG_NA_ITEMS"):
                items = items[:int(_os.environ["DBG_NA_ITEMS"])]

            def na_qk(idx):
                j, h = items[idx]
                lo, hi = na_tiles(j)
                sb_ = idx % 2
                cc, hp = h // 2, h % 2

                def f(e):
                    ins = None
                    for i in range(lo, hi + 1):
                        o = ps[:, 1024 * sb_ + (i - lo) * 128:1024 * sb_ + (i - lo + 1) * 128]
                        e.matmul(o, knam[:, h, i * 128:(i + 1) * 128], qna[:, cc, j * 128:(j + 1) * 128],
                                 start=True, stop=False)
                        ins = e.matmul(o, ident_bf[:, :], nab[:, h * N_NATBL + NA_TBL_OF[(j, i)], :], start=False, stop=True)
                    return ins
                rr = [r.qna[cc][j // 4], r.nabh[h], r.const] + [r.knam[h][i // 4] for i in range(lo, hi + 1)]
                pe(f, rr, [pb[2 * sb_], pb[2 * sb_ + 1]], "naqk")

            def na_exp(idx):
                j, h = items[idx]
                lo, hi = na_tiles(j)
                n = hi - lo + 1
                sb_ = idx % 2
                pt = idx % 3
                def fx(e):
                    return e.activation(out=ptna[pt][:, 0:n * 128], in_=ps[:, 1024 * sb_:1024 * sb_ + n * 128], func=AF.Exp)
                act(fx, [pb[2 * sb_], pb[2 * sb_ + 1]], [r.ptna[pt]], "naexp")

            def na_pv(idx):
                j, h = items[idx]
                lo, hi = na_tiles(j)
                pt = idx % 3
                ob = 4 + 2 * (j % 2)
                ov = oacc_view(ob)

                def f(e):
                    ins = None
                    for i in range(lo, hi + 1):
                        ins = e.matmul(ov[h // 4][:, h % 4, 0:65], ptna[pt][:, (i - lo) * 128:(i - lo + 1) * 128],
                                       vna[:, i, h, :], start=(i == lo), stop=(i == hi))
                    return ins
                pe(f, [r.ptna[pt]] + [r.vna[i] for i in range(lo, hi + 1)], [pb[ob + h // 4]], "napv")
                if h == 7:
                    fin_q.append((idx + 2, lambda: attn_finish(ob, j, 0, False, j)))

            fin_q = []
            na_qk(0)
            for idx in range(len(items)):
                na_exp(idx)
                while fin_q and fin_q[0][0] <= idx:
                    fin_q.pop(0)[1]()
                if idx + 1 < len(items):
                    na_qk(idx + 1)
                na_pv(idx)
                if idx % 8 == 3 and idx // 8 < 8:
                    issue_wout_fold(idx // 8)
            while fin_q:
                fin_q.pop(0)[1]()

            ckpt("P3")
            P.barrier()
            sitems = [(n, g) for n in range(NT) for g in range(2)]

            def sw_deltas(n):
                return [d for d in (-1, 0, 1) if 0 <= n + d < NT]

            def sw_qk(idx):
                n, g = sitems[idx]
                sb_ = idx % 2
                base = 1536 * sb_

                def f(e):
                    ins = None
                    for d in sw_deltas(n):
                        di = d + 1
                        kb = n + d
                        for hh in range(4):
                            hp = hh % 2
                            o = ps[:, base + di * 512 + hh * 128:base + di * 512 + (hh + 1) * 128]
                            ins = e.matmul(o, ktswm[:, 2 * g + hp, kb * 128:(kb + 1) * 128],
                                           qtsw[:, 2 * g + hh // 2, n * 128:(n + 1) * 128], start=True, stop=(d == 0))
                            if d != 0:
                                mi = 0 if d == -1 else 1
                                ins = e.matmul(o, ident_bf[:, :], swm[:, mi, :], start=False, stop=True)
                    return ins
                rr = [r.const, r.qtsw[2 * g][n // 4], r.qtsw[2 * g + 1][n // 4]] + [r.ktswm[2 * g + hp_][(n + d) // 4] for d in sw_deltas(n) for hp_ in range(2)]
                pe(f, rr, [pb[3 * sb_], pb[3 * sb_ + 1], pb[3 * sb_ + 2]], "swqk")

            def sw_exp(idx):
                n, g = sitems[idx]
                sb_ = idx % 2
                base = 1536 * sb_
                ds_ = sw_deltas(n)
                d0, d1 = ds_[0] + 1, ds_[-1] + 1
                def fx(e):
                    ins = None
                    for di in range(d0, d1 + 1):
                        ins = e.activation(out=ptsw[sb_][:, di, :], in_=ps[:, base + di * 512:base + (di + 1) * 512], func=AF.Exp)
                    return ins
                act(fx, [pb[3 * sb_], pb[3 * sb_ + 1], pb[3 * sb_ + 2]], [r.ptsw[sb_]], "swexp")

            def sw_pv(idx):
                n, g = sitems[idx]
                sb_ = idx % 2
                ov = oacc_view(6)
                ds_ = sw_deltas(n)

                def f(e):
                    ins = None
                    for hh in range(4):
                        for d in ds_:
                            ins = e.matmul(ov[g][:, hh, 0:65], ptsw[sb_][:, d + 1, hh * 128:(hh + 1) * 128],
                                           vsw[:, n + d, g, :], start=(d == ds_[0]), stop=(d == ds_[-1]))
                    return ins
                pe(f, [r.ptsw[sb_]] + [r.vsw[n + d] for d in ds_], [pb[6 + g]], "swpv")
                if g == 1:
                    fin_sw.append((idx + 2, attn_finish(6, n, 1, True, n, defer_b=True)))

            fin_sw = []
            sw_qk(0)
            for idx in range(len(sitems)):
                sw_exp(idx)
                while fin_sw and fin_sw[0][0] <= idx:
                    fin_sw.pop(0)[1]()
                if idx + 1 < len(sitems):
                    sw_qk(idx + 1)
                sw_pv(idx)
            while fin_sw:
                fin_sw.pop(0)[1]()

            ckpt("P4")
            P.barrier()
            build_gate_bc(s, 5, gatef_bc, r.gatef)
            p5st = {}

            def p5_s1(t):
                tb = t % 2
                pbf = bank_bf(tb)
                dma("sp", xin5[tb][:, :], x_d[s, t * 128:(t + 1) * 128, :], [], [r.xin[tb]])

                def ft(e):
                    ins = None
                    for c in range(8):
                        ins = e.transpose(pbf[:, c * 128:(c + 1) * 128], cat[:, t, c * 128:(c + 1) * 128], ident_bf[:, :])
                    return ins
                pe(ft, [r.cat[t][0], r.cat[t][1], r.const], [pb[tb]], "catT")
                act(lambda e: e.activation(out=cT[tb][:, :, :].rearrange("p c q -> p (c q)"), in_=pbf, func=AF.Copy),
                    [pb[tb]], [r.cT[tb]], "cT")

            def p5_s2(t):
                tb = t % 2
                mb = 2 + 2 * tb

                def fo(e):
                    ins = None
                    for hf in range(2):
                        for c in range(8):
                            ins = e.matmul(bank(mb + hf), cT[tb][:, c, :], woutg[:, c, hf * 512:(hf + 1) * 512],
                                           start=(c == 0), stop=(c == 7))
                    return ins
                pe(fo, [r.cT[tb], r.woutg], [pb[mb], pb[mb + 1]], "outproj")

                def fx1(e):
                    ins = None
                    for hf in range(2):
                        ins = e.tensor_tensor(x1[:, t, hf * 512:(hf + 1) * 512], xin5[tb][:, hf * 512:(hf + 1) * 512],
                                              bank(mb + hf), ALU.add)
                    return ins
                dve(fx1, [r.xin[tb], pb[mb], pb[mb + 1]], [r.x1[t]], "x1")
                p5st[t] = rms_stats(x1[:, t, :], r.x1[t], xn5[tb], r.xn[tb])

            def p5_s3(t, part, s=s):
                tb = t % 2
                st_, sres_ = p5st[t]
                rms_apply(x1[:, t, :], r.x1[t], st_, sres_, xn5[tb], r.xn[tb], 6 + tb, h2T, r.h2T2[t], t,
                          lambda c: modv[:, s, 3, c:c + 1], lambda c: modv[:, s, 4, c:c + 1], r.modv[s][3:5],
                          evac_dve=True, part=part)

            for it in range(NT + 2):
                if 0 <= it - 2 < NT:
                    p5_s3(it - 2, "xn")
                if it < NT:
                    p5_s1(it)
                if 0 <= it - 1 < NT:
                    p5_s2(it - 1)
                if 0 <= it - 2 < NT:
                    p5_s3(it - 2, "rest")

            ckpt("P5")
            P.barrier()
            dma("sp", gfin_bc[:, :], gfin_d.partition_broadcast(128), [], [r.gfin])
            if True:
                def fz(e):
                    for g_ in Gb:
                        e.memset(g_[:, 0:1], 0.0)
                        ins = e.memset(g_[:, 2049:2050], 0.0)
                    return ins
                pool(fz, [], r.G, "gzero")
            wup_v = wup_d.rearrange("(k p) f -> p k f", p=128)
            dcnt = [0]
            group_of = {}
            for gi_, (c0_, c1_) in enumerate(GROUPS):
                for c_ in range(c0_, c1_):
                    group_of[c_] = gi_

            def wup_load(c):
                wb = c % 3
                dma("pool", wup_sb[wb][:, :, 128:256], wup_v[:, :, DFF + c * 128:DFF + (c + 1) * 128], [], [r.wupg[wb]])
                dma("pool", wup_sb[wb][:, :, 0:128], wup_v[:, :, c * 128:(c + 1) * 128], [], [r.wup[wb]])

            def issue_gate(c):
                wb, gb = c % 3, c % 2
                for b4 in range(4):
                    gk = b4 % 2

                    def fg(e, wb=wb, b4=b4, gk=gk):
                        ins = None
                        for k in range(8):
                            ins = e.matmul(bank(gk), wup_sb[wb][:, k, 128:256], h2T[:, k, b4 * 512:(b4 + 1) * 512],
                                           start=(k == 0), stop=(k == 7))
                        return ins
                    pe(fg, [r.wupg[wb]] + r.h2T[8 * b4:8 * b4 + 8], [pb[gk]], "gate")
                    act(lambda e, gb=gb, b4=b4, gk=gk: e.activation(out=Gb[gb][:, 1 + b4 * 512:1 + (b4 + 1) * 512],
                                                                   in_=bank(gk), func=AF.Copy),
                        [pb[gk]], [r.G[gb]], "gevac")

            def issue_conv(c):
                gb = c % 2
                t1, t1r = T1[c % 2], r.T1[c % 2]
                cw0 = paramsT[:, 94 + c:95 + c]
                cw1 = paramsT[:, 94 + 22 + c:95 + 22 + c]
                cw2 = paramsT[:, 94 + 44 + c:95 + 44 + c]
                cbb = paramsT[:, 72 + c:73 + c]
                pool(lambda e: e.tensor_scalar(t1[:, :], Gb[gb][:, 0:2048], cw0, cbb, ALU.mult, ALU.add),
                     [r.G[gb], r.params], [t1r], "conv0")
                dve(lambda e: e.scalar_tensor_tensor(t1[:, :], Gb[gb][:, 1:2049], cw1, t1[:, :], ALU.mult, ALU.add),
                    [r.G[gb], r.params, t1r], [t1r], "conv1")
                dve(lambda e: e.scalar_tensor_tensor(t1[:, :], Gb[gb][:, 2:2050], cw2, t1[:, :], ALU.mult, ALU.add),
                    [r.G[gb], r.params, t1r], [t1r], "conv2")

            def issue_silu(c):
                t1, t1r = T1[c % 2], r.T1[c % 2]
                act(lambda e: e.activation(out=t1[:, :], in_=t1[:, :], func=AF.Silu), [t1r], [t1r], "silu")

            def issue_val(c):
                wb = c % 3
                t1, t1r = T1[c % 2], r.T1[c % 2]
                ci = c - GROUPS[group_of[c]][0]
                for b4 in range(4):
                    vk = 2 + b4 % 2

                    def fvv(e, wb=wb, b4=b4, vk=vk):
                        ins = None
                        for k in range(8):
                            ins = e.matmul(bank(vk), wup_sb[wb][:, k, 0:128], h2T[:, k, b4 * 512:(b4 + 1) * 512],
                                           start=(k == 0), stop=(k == 7))
                        return ins
                    pe(fvv, [r.wup[wb]] + r.h2T[8 * b4:8 * b4 + 8], [pb[vk]], "val")
                    dve(lambda e, ci=ci, b4=b4, vk=vk: e.tensor_tensor(uT[:, ci, b4 * 512:(b4 + 1) * 512],
                                                                      t1[:, b4 * 512:(b4 + 1) * 512], bank(vk), ALU.mult),
                        [t1r, pb[vk]], [r.uT[ci][b4]], "umul")

            def issue_fold(c):
                ci = c - GROUPS[group_of[c]][0]
                wb = c % 2
                dma("sp", wdstg[wb][:, :], wdn_d[c * 128:(c + 1) * 128, :], [], [r.wdstg[wb]])
                dve(lambda e: e.tensor_tensor(wdng[:, ci, :], wdstg[wb][:, :], gatef_bc[:, :], ALU.mult),
                    [r.wdstg[wb], r.gatef], [r.wdng[ci]], "wdng")

            fin6 = []

            def issue_down(gi):
                c0, c1 = GROUPS[gi]
                last = gi == len(GROUPS) - 1
                ng = c1 - c0
                for t in range(NT):
                    for hf in range(2):
                        dk = 4 + dcnt[0] % 4
                        dcnt[0] += 1

                        def fd(e, t=t, hf=hf, dk=dk, ng=ng):
                            ins = None
                            for ci in range(ng):
                                ins = e.matmul(bank(dk), uT[:, ci, t * 128:(t + 1) * 128], wdng[:, ci, hf * 512:(hf + 1) * 512],
                                               start=(ci == 0), stop=(ci == ng - 1))
                            return ins
                        pe(fd, [r.uT[ci][t // 4] for ci in range(ng)] + r.wdng[0:ng], [pb[dk]], "down")
                        dve(lambda e, t=t, hf=hf, dk=dk: e.tensor_tensor(x1[:, t, hf * 512:(hf + 1) * 512],
                                                                        x1[:, t, hf * 512:(hf + 1) * 512], bank(dk), ALU.add),
                            [pb[dk], r.x1[t]], [r.x1[t]], "x2acc")
                    if last:
                        def fstats(t=t):
                            ob = t % 2
                            st, sres = new_stat()
                            act(lambda e: e.activation(out=ostg[ob][:, :], in_=x1[:, t, :], func=AF.Square, accum_out=st[:, 0:1]),
                                [r.x1[t]], [sres, r.ostg[ob]], "fssq")
                            rstd_from_ssq(st, sres, 1024.0)
                            return st, sres

                        def ffinal(stp, t=t):
                            ob = t % 2
                            st, sres = stp
                            dve(lambda e: e.scalar_tensor_tensor(ostg[ob][:, :], x1[:, t, :], st[:, 1:2],
                                                                 gfin_bc[:, :], ALU.mult, ALU.mult),
                                [r.x1[t], sres, r.gfin], [r.ostg[ob]], "final")
                            out_dmas.append(dma("sp", out_d[s, t * 128:(t + 1) * 128, :], ostg[ob][:, :], [r.ostg[ob]], []))
                        fin6.append([t, fstats, ffinal, None])
                        for ent in fin6:
                            if ent[3] is None and ent[0] <= t - 1:
                                ent[3] = ent[1]()
                        while fin6 and fin6[0][0] <= t - 2:
                            ent = fin6.pop(0)
                            ent[2](ent[3])
                if last:
                    for ent in fin6:
                        if ent[3] is None:
                            ent[3] = ent[1]()
                    while fin6:
                        ent = fin6.pop(0)
                        ent[2](ent[3])

            wup_load(0)
            wup_load(1)
            wup_load(2)
            for c in range(NCH):
                if c >= 2 and c + 1 < NCH:
                    wup_load(c + 1)
                issue_gate(c)
                if c > 0:
                    issue_silu(c - 1)
                    issue_val(c - 1)
                issue_conv(c)
                if c > 0 and group_of[c - 1] != group_of[c]:
                    issue_down(group_of[c - 1])
                issue_fold(c)
            issue_silu(NCH - 1)
            issue_val(NCH - 1)
            issue_down(len(GROUPS) - 1)

    except _Stop:
        pass

    P.barrier()

    with ExitStack() as es:
        es.enter_context(nc.allow_low_precision("bf16 matmul operands, fp32 accumulation"))
        es.enter_context(nc.allow_non_contiguous_dma("small strided parameter loads"))
        eng_sems = {e: es.enter_context(nc.semaphore(f"sem_{e}")) for e in ENGS}
        dma_sems = {q: [es.enter_context(nc.semaphore(f"dsem_{q}{i}")) for i in range(n)]
                    for q, n in P.n_dma_sems.items()}
        block = es.enter_context(nc.Block())
        P.emit(nc, block, eng_sems, dma_sems)
    return nc


_CONSTS = {}


def _consts():
    if not _CONSTS:
        C, S = build_rope()
        _CONSTS.update(dict(rope_c=C, rope_s=S, swmask=build_sw_mask(),
                            ident=np.eye(128, dtype=np.float32), perm=build_perm()))
    return _CONSTS


def make_in_maps(x, c, w_ada, b_ada, g_attn, w_in, na_rpb, sw_sink, g_na_out, g_sw_out,
                 w_out, g_ffn, w_up, conv_w, conv_b, w_down, g_final, cores=range(N_CORES)):
    f = lambda a: np.ascontiguousarray(np.asarray(a, dtype=np.float32))
    x = f(x); c = f(c)
    cst = _consts()
    nabias = build_na_bias(f(na_rpb)[0])
    base_rows = [f(b_ada)[0].reshape(48, 128), f(g_attn)[0].reshape(8, 128), f(g_ffn)[0].reshape(8, 128),
                 f(g_na_out)[0].reshape(4, 128), f(g_sw_out)[0].reshape(4, 128),
                 f(conv_b)[0].reshape(22, 128), f(conv_w)[0].reshape(66, 128)]
    shared = dict(w_ada=f(w_ada)[0], w_in=f(w_in)[0], w_out=f(w_out)[0], w_up=f(w_up)[0], w_down=f(w_down)[0],
                  g_final=f(g_final), sw_sink=f(sw_sink)[0], nabias=nabias, **cst)
    in_maps = []
    for core in cores:
        b0 = core * SEQ_PER_CORE
        rows = base_rows + [c[b0].reshape(8, 128), c[b0 + 1].reshape(8, 128)]
        params = np.zeros((256, 128), np.float32)
        cat_rows = np.concatenate(rows, axis=0)
        params[:cat_rows.shape[0]] = cat_rows
        m = dict(shared)
        m["x"] = np.ascontiguousarray(x[b0:b0 + SEQ_PER_CORE])
        m["params"] = params
        in_maps.append(m)
    return in_maps


def kernel(x, c, w_ada, b_ada, g_attn, w_in, na_rpb, sw_sink, g_na_out, g_sw_out,
           w_out, g_ffn, w_up, conv_w, conv_b, w_down, g_final):
    in_maps = make_in_maps(x, c, w_ada, b_ada, g_attn, w_in, na_rpb, sw_sink, g_na_out, g_sw_out,
                           w_out, g_ffn, w_up, conv_w, conv_b, w_down, g_final)
    nc = build_program()
    res = run_bass_kernel_spmd(nc, in_maps, core_ids=list(range(N_CORES)))
    out = np.concatenate([np.asarray(r["out"]) for r in res.results], axis=0)
    return out.astype(np.float32)
```

```python
import numpy as np
from contextlib import ExitStack

import concourse.bass as bass
import concourse.mybir as mybir
from concourse.bass_utils import run_bass_kernel_spmd

F32 = mybir.dt.float32
BF16 = mybir.dt.bfloat16
AF = mybir.ActivationFunctionType
ALU = mybir.AluOpType

D = 1024
SEQ = 2048
NT = 16
DFF = 2816
NCH = 22
EPS = 1e-6
NEG = -30000.0
N_CORES = 8
SEQ_PER_CORE = 2
GROUPS = [(0, 6), (6, 12), (12, 17), (17, 22)]


class Res:
    __slots__ = ("name", "w", "rd", "rdma")

    def __init__(self, name):
        self.name = name
        self.w = None
        self.rd = {}
        self.rdma = []


class Op:
    __slots__ = ("eng", "fn", "deps", "dma", "sem", "semval", "signals", "sigidx", "name")


ENGS = ["pe", "act", "dve", "pool", "sp"]


class Prog:
    def __init__(self, n_dma_sems):
        self.ops = []
        self.n_dma_sems = n_dma_sems
        self.dma_last = {q: [None] * n for q, n in n_dma_sems.items()}
        self.dma_cnt = {q: [0] * n for q, n in n_dma_sems.items()}
        self.dma_rr = {q: 0 for q in n_dma_sems}
        self.last_op = {e: None for e in ENGS}
        self.pending_dma = []

    def add(self, eng, fn, reads=(), writes=(), dma=False, name="", extra_deps=()):
        op = Op()
        op.eng, op.fn, op.dma, op.name = eng, fn, dma, name
        op.signals = False
        op.sigidx = 0
        op.sem = None
        op.semval = 0
        deps = list(extra_deps)
        for r in reads:
            if r.w is not None:
                deps.append(r.w)
        for w in writes:
            if w.w is not None:
                deps.append(w.w)
            deps.extend(w.rd.values())
            deps.extend(w.rdma)
        if dma:
            q = eng
            n = self.n_dma_sems[q]
            j = self.dma_rr[q]
            self.dma_rr[q] = (j + 1) % n
            prev = self.dma_last[q][j]
            if prev is not None:
                deps.append(prev)
            self.dma_cnt[q][j] += 1
            op.sem = (q, j)
            op.semval = 16 * self.dma_cnt[q][j]
            self.dma_last[q][j] = op
            self.pending_dma.append(op)
        for r in reads:
            if dma:
                r.rdma.append(op)
            else:
                r.rd[eng] = op
        for w in writes:
            w.w = op
            w.rd = {}
            w.rdma = []
        seen = set()
        dl = []
        for d in deps:
            if d is op or d is None or id(d) in seen:
                continue
            seen.add(id(d))
            dl.append(d)
        op.deps = dl
        self.ops.append(op)
        self.last_op[eng] = op
        return op

    def barrier(self):
        lasts = dict(self.last_op)
        pend = list(self.pending_dma)
        self.pending_dma = []
        for e in ENGS:
            deps = [o for ee, o in lasts.items() if ee != e and o is not None] + pend
            self.add(e, None, extra_deps=deps, name="barrier")

    def emit(self, nc, block, eng_sems, dma_sems):
        for op in self.ops:
            for d in op.deps:
                if not d.dma and (d.eng != op.eng or op.dma or d.eng != "pe"):
                    d.signals = True
        cnt = {e: 0 for e in ENGS}
        for op in self.ops:
            if (not op.dma) and op.signals:
                if op.fn is None:
                    op.signals = False
                    op.sigidx = cnt[op.eng]
                else:
                    cnt[op.eng] += 1
                    op.sigidx = cnt[op.eng]
        by_eng = {e: [o for o in self.ops if o.eng == e] for e in ENGS}

        def run(e, eng):
            waited = {}
            for op in by_eng[e]:
                need = {}
                for d in op.deps:
                    if d.dma:
                        key = ("dma",) + d.sem
                        val = d.semval
                    else:
                        if d.eng == op.eng and d.eng == "pe" and not op.dma:
                            continue
                        key = ("eng", d.eng)
                        val = d.sigidx
                    if val > need.get(key, 0):
                        need[key] = val
                for key, val in need.items():
                    if waited.get(key, 0) < val:
                        sem = eng_sems[key[1]] if key[0] == "eng" else dma_sems[key[1]][key[2]]
                        eng.wait_ge(sem, val)
                        waited[key] = val
                if op.fn is None:
                    continue
                ins = op.fn(eng)
                if op.dma:
                    ins.then_inc(dma_sems[op.sem[0]][op.sem[1]], 16)
                elif op.signals:
                    ins.then_inc(eng_sems[e], 1)

        @block.tensor
        def _(eng):
            run("pe", eng)

        @block.scalar
        def _(eng):
            run("act", eng)

        @block.vector
        def _(eng):
            run("dve", eng)

        @block.gpsimd
        def _(eng):
            run("pool", eng)

        @block.sync
        def _(eng):
            run("sp", eng)


def na_tiles(j):
    lo = min(max(j - 2, 0), 12)
    hi = max(min(j + 2, 15), 3)
    return lo, hi


def na_table_struct():
    sigs = {}
    tbl_of = {}
    for j in range(16):
        lo, hi = na_tiles(j)
        for i in range(lo, hi + 1):
            sig = []
            for qr in (2 * j, 2 * j + 1):
                rs = min(max(qr - 4, 0), 24)
                for kr in (2 * i, 2 * i + 1):
                    sig.append((kr - qr) if (rs <= kr < rs + 8) else None)
            sig = tuple(sig)
            if sig not in sigs:
                sigs[sig] = len(sigs)
            tbl_of[(j, i)] = sigs[sig]
    return tbl_of, sigs


NA_TBL_OF, NA_SIGS = na_table_struct()
N_NATBL = len(NA_SIGS)


def build_na_bias(rpb):
    kc = np.arange(64)[:, None]
    qc = np.arange(64)[None, :]
    cs = np.clip(qc - 8, 0, 48)
    colok = (kc >= cs) & (kc < cs + 16)
    dc = np.clip(kc - qc + 15, 0, 30)
    out = np.full((128, 8, N_NATBL, 128), NEG, dtype=np.float32)
    for sig, t in NA_SIGS.items():
        for qri in range(2):
            for kri in range(2):
                off = sig[qri * 2 + kri]
                if off is None:
                    continue
                for h in range(8):
                    blk = np.where(colok, rpb[h, off + 7][dc], np.float32(NEG))
                    out[kri * 64:(kri + 1) * 64, h, t, qri * 64:(qri + 1) * 64] = blk
    return np.ascontiguousarray(out.reshape(128, 8 * N_NATBL, 128))


def build_sw_mask():
    kk = np.arange(128)[:, None]
    qq = np.arange(128)[None, :]
    m = np.zeros((128, 3, 128), dtype=np.float32)
    m[:, 0, :] = np.where(kk >= qq, 0.0, NEG)
    m[:, 1, :] = np.where(kk <= qq, 0.0, NEG)
    return m


def build_rope():
    half = 32
    inv = (np.float32(10000.0) ** (-(np.arange(half, dtype=np.float32) / np.float32(half)))).astype(np.float32)
    pos = np.arange(SEQ, dtype=np.float32)
    ang = (pos[:, None] * inv[None, :]).astype(np.float32)
    cos = np.cos(ang).astype(np.float32).T
    sin = np.sin(ang).astype(np.float32).T
    C = np.zeros((128, SEQ), np.float32)
    S = np.zeros((128, SEQ), np.float32)
    for p in range(128):
        d = p % 64
        i = d % 32
        C[p] = cos[i]
        S[p] = -sin[i] if d < 32 else sin[i]
    return C, S


def build_perm():
    pm = np.zeros((128, 128), np.float32)
    for m in range(128):
        hb, d = divmod(m, 64)
        pm[hb * 64 + (d + 32) % 64, m] = 1.0
    return pm


class _Stop(Exception):
    pass


def build_program(stop_after=None, dumps=()):
    nc = bass.Bass("TRN2", target_bir_lowering=False)
    P = Prog({"sp": 12, "pool": 10})
    dump_src = {}

    def ckpt(tag):
        if stop_after != tag:
            return
        P.barrier()
        for nm in dumps:
            src = dump_src[nm]
            dt_ = nc.dram_tensor("dbg_" + nm, list(src.shape), src.dtype, kind="ExternalOutput").ap()
            P.add("sp", lambda e, o=dt_, i=src: e.dma_start(out=o, in_=i), [], [], dma=True, name="dump")
        raise _Stop()

    def din(name, shape):
        return nc.dram_tensor(name, list(shape), F32, kind="ExternalInput").ap()

    x_d = din("x", (SEQ_PER_CORE, SEQ, D))
    wada_d = din("w_ada", (D, 6 * D))
    win_d = din("w_in", (D, 2304))
    wout_d = din("w_out", (D, D))
    wup_d = din("w_up", (D, 2 * DFF))
    wdn_d = din("w_down", (DFF, D))
    params_d = din("params", (256, 128))
    gfin_d = din("g_final", (D,))
    sink_d = din("sw_sink", (8,))
    nab_d = din("nabias", (128, 8 * N_NATBL, 128))
    swm_d = din("swmask", (128, 3, 128))
    ropec_d = din("rope_c", (128, SEQ))
    ropes_d = din("rope_s", (128, SEQ))
    ident_d = din("ident", (128, 128))
    perm_d = din("perm", (128, 128))
    out_d = nc.dram_tensor("out", [SEQ_PER_CORE, SEQ, D], F32, kind="ExternalOutput").ap()

    cur = [16512]
    LIMIT = 229344

    def alloc(name, shape, dt, at=None):
        nbytes = int(np.prod(shape[1:])) * (4 if dt == F32 else 2)
        nbytes = (nbytes + 31) // 32 * 32
        if at is None:
            off = cur[0]
            cur[0] += nbytes
        else:
            off = at
        assert off + nbytes <= LIMIT, (name, off, nbytes)
        return nc.alloc_sbuf_tensor_at(name, list(shape), dt, offset=off)

    ident_bf = alloc("ident_bf", [128, 128], BF16)
    ident_f = alloc("ident_f", [128, 128], F32)
    perm_f = alloc("perm_f", [128, 128], F32)
    paramsT = alloc("paramsT", [128, 256], F32)
    modT = alloc("modT", [128, 48, 2], F32)
    modv = alloc("modv", [128, 2, 6, 8], F32)
    esink = alloc("esink", [128, 8], F32)
    swm = alloc("swm", [128, 3, 128], BF16)
    scT = alloc("scT", [128, 2, 8], BF16)
    pstage = [alloc(f"pstage{i}", [128, 128], F32) for i in range(2)]
    tmpf = [alloc(f"tmpf{i}", [128, 128], F32) for i in range(2)]
    NSTAT = 8
    stat = alloc("stat", [128, NSTAT, 4], F32)
    rden_t = alloc("rden_t", [128, 2, 8], F32)
    dtmp_t = alloc("dtmp_t", [128, 2, 8], F32)
    tmp8 = alloc("tmp8", [128, 8], F32)
    tmp8b = alloc("tmp8b", [128, 8], F32)
    epsc = alloc("epsc", [128, 8], F32)
    A0 = cur[0]
    ARENA = LIMIT - A0

    def at(off):
        return A0 + off

    R0, R1, R2, R3 = 0, 32768, 114944, 135488
    hT = alloc("hT", [128, 8, SEQ], BF16, at(R0))
    qna = alloc("qna", [128, 4, SEQ], BF16, at(R1))
    knam = alloc("knam", [128, 8, SEQ], BF16, at(R1 + 16384))
    vna = alloc("vna", [128, NT, 8, 65], BF16, at(R1 + 49152))
    qtsw = alloc("qtsw", [128, 4, SEQ], BF16, at(R1 + 49152 + 16640))
    ktswm = alloc("ktswm", [128, 4, SEQ], BF16, at(R2))
    vsw = alloc("vsw", [128, NT, 2, 65], BF16, at(R2 + 16384))
    win_sb = alloc("win_sb", [128, 8, 2432], BF16, at(R3))
    ropec = alloc("ropec", [128, SEQ], F32, at(R3 + 38912))
    ropes = alloc("ropes", [128, SEQ], F32, at(R3 + 38912 + 8192))
    T3 = R3 + 38912 + 16384
    xin = [alloc(f"xin{i}", [128, 1024], F32, at(T3 + 4096 * i)) for i in range(2)]
    xn = [alloc(f"xn{i}", [128, 1024], BF16, at(T3 + 8192 + 2048 * i)) for i in range(2)]
    qf = [alloc(f"qf{i}", [128, 512], F32, at(T3 + 2048 * i)) for i in range(2)]
    ra = [alloc(f"ra{i}", [128, 512], F32, at(T3 + 4096 + 2048 * i)) for i in range(2)]
    wada_sb = [alloc(f"wada{i}", [128, 8, 512], BF16, at(R3 + 8192 * i)) for i in range(2)]
    wada2_sb = [alloc(f"wada2_{i}", [128, 8, 512], BF16, at(R1 + 49152 + 16640 + 8192 * i)) for i in range(2)]
    nab = alloc("nab", [128, 8 * N_NATBL, 128], BF16, at(R0))
    cat = alloc("cat", [128, NT, 1024], BF16, at(R3))
    ptna = [alloc(f"ptna{i}", [128, 640], BF16, at(R3 + 32768 + 1280 * i)) for i in range(3)]
    ptsw = [alloc(f"ptsw{i}", [128, 3, 512], BF16, at(R3 + 32768 + 3072 * i)) for i in range(2)]
    osb = [alloc(f"osb{i}", [128, 512], F32, at(R3 + 32768 + 6144 + 2048 * i)) for i in range(2)]
    woutg = alloc("woutg", [128, 8, 1024], BF16, at(R3 + 45056))
    wstg = [alloc(f"wstg{i}", [128, 1024], F32, at(R3 + 61440 + 4096 * i)) for i in range(2)]
    cT = [alloc(f"cT{i}", [128, 8, 128], BF16, at(R3 + 32768 + 2048 * i)) for i in range(2)]
    xin5 = [alloc(f"xin5_{i}", [128, 1024], F32, at(R3 + 36864 + 4096 * i)) for i in range(2)]
    xn5 = [alloc(f"xn5_{i}", [128, 1024], BF16, at(R1 + 65536 + 12288 + 2048 * i)) for i in range(2)]
    gatea_bc = alloc("gatea_bc", [128, 1024], F32, at(R0 + 18432))
    gatef_bc = alloc("gatef_bc", [128, 1024], F32, at(R1 + 65536 + 4096))
    gfin_bc = alloc("gfin_bc", [128, 1024], F32, at(R1 + 65536 + 8192))
    x1 = alloc("x1", [128, NT, 1024], F32, at(R1))
    h2T = alloc("h2T", [128, 8, SEQ], BF16, at(R0))
    wdng = alloc("wdng", [128, 6, 1024], BF16, at(R2))
    uT = alloc("uT", [128, 6, SEQ], BF16, at(R3))
    Gb = [alloc(f"Gb{i}", [128, 2052], F32, at(R3 + 24576 + 8224 * i)) for i in range(2)]
    T1 = [alloc(f"T1_{i}", [128, SEQ], F32, at(R3 + 41024 + 8192 * i)) for i in range(2)]
    wup_sb = [alloc(f"wup{i}", [128, 8, 256], BF16, at(R3 + 57408 + 4096 * i)) for i in range(3)]
    wdstg = [alloc(f"wdstg{i}", [128, 1024], F32, at(R2 + 12288 + 4096 * i)) for i in range(2)]
    ostg = [alloc("ostg0", [128, 1024], F32, at(R1 + 65536 + 12288)), alloc("ostg1", [128, 1024], F32, at(R1 + 65536))]

    ps = nc.alloc_psum_tensor("ps", [128, 4096], F32)

    def bank(b, n=1):
        return ps[:, 512 * b:512 * (b + n)]

    def bank_bf(b):
        return ps[:, 512 * b:512 * (b + 1)].bitcast(BF16)

    pb = [Res(f"psum{b}") for b in range(8)]

    class RS:
        pass

    r = RS()
    r.const = Res("const")
    r.params = Res("params")
    r.mod = Res("mod")
    r.gatea = Res("gatea")
    r.gatef = Res("gatef")
    r.stat = [Res(f"stat{i}") for i in range(NSTAT)]
    r.rden = [Res("rden0"), Res("rden1")]
    r.dtmp = [Res("dtmp0"), Res("dtmp1")]
    r.tmp8 = Res("tmp8")
    r.tmp8b = Res("tmp8b")
    r.modv = [[Res(f"modv{b}_{k}") for k in range(6)] for b in range(2)]
    r.hT2 = [[Res(f"hT{t}a"), Res(f"hT{t}b")] for t in range(NT)]
    r.hT = [x for p in r.hT2 for x in p]
    r.qna = [[Res(f"qna{c}_{b}") for b in range(4)] for c in range(4)]
    r.knam = [[Res(f"knam{h}_{b}") for b in range(4)] for h in range(8)]
    r.gfin = Res("gfin")
    r.vna = [Res(f"vna{t}") for t in range(NT)]
    r.qtsw = [[Res(f"qtsw{c}_{b}") for b in range(4)] for c in range(4)]
    r.ktswm = [[Res(f"ktswm{c}_{b}") for b in range(4)] for c in range(4)]
    r.vsw = [Res(f"vsw{t}") for t in range(NT)]
    r.winb = {k: Res("win_" + k) for k in ("qa", "ka", "va", "vb", "qb", "kb0", "kb1", "kb2", "kb3")}
    r.ropeC = Res("ropeC")
    r.ropeS = Res("ropeS")
    r.xin = [Res("xin0"), Res("xin1")]
    r.xn = [Res("xn0"), Res("xn1")]
    r.qf = [Res("qf0"), Res("qf1")]
    r.ra = [Res("ra0"), Res("ra1")]
    r.wada = [Res("wada0"), Res("wada1")]
    r.wada2 = [Res("wada2_0"), Res("wada2_1")]
    r.mod2 = Res("mod2")
    r.pstage = [Res("pstage0"), Res("pstage1")]
    r.tmpf = [Res("tmpf0"), Res("tmpf1")]
    r.nabh = [Res(f"nab{h}") for h in range(8)]
    r.cat = [[Res(f"cat{t}_{h}") for h in range(2)] for t in range(NT)]
    r.ptna = [Res(f"ptna{i}") for i in range(3)]
    r.ptsw = [Res(f"ptsw{i}") for i in range(2)]
    r.osb = [Res("osb0"), Res("osb1")]
    r.woutg = Res("woutg")
    r.wstg = [Res("wstg0"), Res("wstg1")]
    r.cT = [Res("cT0"), Res("cT1")]
    r.x1 = [Res(f"x1_{t}") for t in range(NT)]
    r.h2T2 = [[Res(f"h2T{t}a"), Res(f"h2T{t}b")] for t in range(NT)]
    r.h2T = [x for p in r.h2T2 for x in p]
    r.wdng = [Res(f"wdng{i}") for i in range(6)]
    r.uT = [[Res(f"uT{c}_{b}") for b in range(4)] for c in range(6)]
    r.G = [Res("G0"), Res("G1")]
    r.T1 = [Res("T1_0"), Res("T1_1")]
    r.wup = [Res("wup0"), Res("wup1"), Res("wup2")]
    r.wupg = [Res("wupg0"), Res("wupg1"), Res("wupg2")]
    r.wdstg = [Res("wdstg0"), Res("wdstg1")]
    r.ostg = [Res("ostg0"), Res("ostg1")]

    statc = [0]

    def new_stat():
        i = statc[0] % NSTAT
        statc[0] += 1
        return stat[:, i, :], r.stat[i]

    def dma(q, out, in_, reads=(), writes=(), name=""):
        return P.add(q, lambda e, o=out, i=in_: e.dma_start(out=o, in_=i), reads, writes, dma=True, name=name)

    def act(fn, reads, writes, name=""):
        return P.add("act", fn, reads, writes, name=name)

    def dve(fn, reads, writes, name=""):
        return P.add("dve", fn, reads, writes, name=name)

    def pool(fn, reads, writes, name=""):
        return P.add("pool", fn, reads, writes, name=name)

    def pe(fn, reads, writes, name=""):
        return P.add("pe", fn, reads, writes, name=name)

    def rstd_from_ssq(st, sres, n):
        act(lambda e, st=st: e.activation(out=st[:, 2:3], in_=st[:, 0:1], func=AF.Ln, scale=1.0 / n, bias=epsc[:, 0:1]),
            [sres, r.const], [sres], "rstd_ln")
        act(lambda e, st=st: e.activation(out=st[:, 1:2], in_=st[:, 2:3], func=AF.Exp, scale=-0.5),
            [sres], [sres], "rstd_exp")

    def rms_stats(src_ap, src_res, xnbuf, xnres):
        st, sres = new_stat()

        def f1(e, st=st):
            return e.activation(out=xnbuf[:, :], in_=src_ap, func=AF.Square, accum_out=st[:, 0:1])
        act(f1, [src_res], [sres, xnres], "ssq")
        rstd_from_ssq(st, sres, 1024.0)
        return st, sres

    def rms_apply(src_ap, src_res, st, sres, xnbuf, xnres, pstb, dstT, dst_res, t, Acol, Bcol, mres, evac_dve=False,
                  part="all"):
        if part in ("all", "xn"):
            dve(lambda e, st=st: e.tensor_scalar(xnbuf[:, :], src_ap, st[:, 1:2], None, ALU.mult),
                [src_res, sres], [xnres], "xn")
        if part == "xn":
            return
        pbf = bank_bf(pstb)

        def ft(e):
            ins = None
            for c in range(8):
                ins = e.transpose(pbf[:, c * 128:(c + 1) * 128], xnbuf[:, c * 128:(c + 1) * 128], ident_bf[:, :])
            return ins
        pe(ft, [xnres, r.const], [pb[pstb]], "xnT")

        def fe(e):
            ins = None
            for c in range(8):
                ins = e.activation(out=dstT[:, c, t * 128:(t + 1) * 128], in_=pbf[:, c * 128:(c + 1) * 128],
                                   func=AF.Identity, bias=Bcol(c), scale=Acol(c))
            return ins
        if evac_dve:
            def fe2(e):
                ins = None
                for c in range(8):
                    ins = e.tensor_scalar(dstT[:, c, t * 128:(t + 1) * 128], pbf[:, c * 128:(c + 1) * 128],
                                          Acol(c), Bcol(c), ALU.mult, ALU.add)
                return ins
            dve(fe2, [pb[pstb]] + list(mres), list(dst_res), "hT")
        else:
            act(fe, [pb[pstb]] + list(mres), list(dst_res), "hT")

    dump_src.update(dict(paramsT=paramsT[:, :], modT=modT[:, :, :], modv=modv[:, :, :, :], hT=hT[:, :, :],
                         gatea=gatea_bc[:, :], gatef=gatef_bc[:, :], qna=qna[:, :, :], knam=knam[:, :, :], vna=vna[:, :, :, :],
                         qtsw=qtsw[:, :, :], ktswm=ktswm[:, :, :], vsw=vsw[:, :, :, :], cat=cat[:, :, :],
                         x1=x1[:, :, :], h2T=h2T[:, :, :], esink=esink[:, :], scT=scT[:, :, :]))
    try:
        dve(lambda e: e.memset(epsc[:, :], EPS), [], [r.const], "eps")
        dma("sp", ident_f[:, :], ident_d, [], [r.const])
        dma("sp", perm_f[:, :], perm_d, [], [r.const])
        dma("pool", ident_bf[:, :], ident_d, [], [r.const])
        dma("pool", swm[:, :, :], swm_d, [], [r.const])
        dma("sp", esink[:, :], sink_d.partition_broadcast(128), [], [r.const])
        act(lambda e: e.activation(out=esink[:, :], in_=esink[:, :], func=AF.Exp), [r.const], [r.const], "esink")
        for u in range(2):
            dma("sp", pstage[u][:, :], params_d[u * 128:(u + 1) * 128, :], [], [r.pstage[u]])
            pe(lambda e, u=u: e.transpose(bank(1 + u)[:, 0:128], pstage[u][:, :], ident_f[:, :]),
               [r.pstage[u], r.const], [pb[1 + u]], "paramsT")
            dve(lambda e, u=u: e.tensor_copy(paramsT[:, u * 128:(u + 1) * 128], bank(1 + u)[:, 0:128]),
                [pb[1 + u]], [r.params])
        act(lambda e: e.activation(out=scT[:, :, :].rearrange("p b k -> p (b k)"), in_=paramsT[:, 160:176], func=AF.Silu),
            [r.params], [r.mod], "silu_c")
        wada_v = wada_d.rearrange("(k p) f -> p k f", p=128)
        def mod_tile(ct, bufs, bres, accb):
            wb = ct % 2
            dma("pool", bufs[wb][:, :, :], wada_v[:, :, ct * 512:(ct + 1) * 512], [], [bres[wb]])

            def fm(e):
                ins = None
                for fcl in range(4):
                    fc = ct * 4 + fcl
                    for k in range(8):
                        ins = e.matmul(bank(accb)[:, fc * 2:fc * 2 + 2], bufs[wb][:, k, fcl * 128:(fcl + 1) * 128],
                                       scT[:, :, k], start=(k == 0), stop=(k == 7))
                return ins
            pe(fm, [bres[wb], r.mod], [pb[accb]], "modmm")

        def mod_finish(f0, f1, accb, mres):
            dve(lambda e: e.tensor_tensor(modT[:, f0:f1, :], bank(accb)[:, 2 * f0:2 * f1].rearrange("p (f b) -> p f b", b=2),
                                          paramsT[:, f0:f1].unsqueeze(2).broadcast_to([128, f1 - f0, 2]), ALU.add),
                [pb[accb], r.params], [mres], "modT")

        for ct in range(4):
            mod_tile(ct, wada_sb, r.wada, 0)
        mod_finish(0, 16, 0, r.mod)
        for b in range(2):
            ta = tmp8[:, 0:8]
            dve(lambda e, b=b, ta=ta: e.tensor_scalar(ta, modT[:, 8:16, b], 1.0, None, ALU.add), [r.mod], [r.tmp8], "m1")
            dve(lambda e, b=b, ta=ta: e.tensor_tensor(modv[:, b, 0, :], ta, paramsT[:, 48:56], ALU.mult),
                [r.tmp8, r.params], [r.modv[b][0]], "m2")
            dve(lambda e, b=b: e.tensor_copy(modv[:, b, 1, :], modT[:, 0:8, b]), [r.mod], [r.modv[b][1]], "m3")

        def mod_rest_finish():
            mod_finish(16, 48, 3, r.mod2)
            for b in range(2):
                tb_ = tmp8b[:, 0:8]
                dve(lambda e, b=b: e.tensor_copy(modv[:, b, 2, :], modT[:, 16:24, b]), [r.mod2], [r.modv[b][2]], "m4")
                dve(lambda e, b=b, tb_=tb_: e.tensor_scalar(tb_, modT[:, 32:40, b], 1.0, None, ALU.add), [r.mod2], [r.tmp8b], "m5")
                dve(lambda e, b=b, tb_=tb_: e.tensor_tensor(modv[:, b, 3, :], tb_, paramsT[:, 56:64], ALU.mult),
                    [r.tmp8b, r.params], [r.modv[b][3]], "m6")
                dve(lambda e, b=b: e.tensor_copy(modv[:, b, 4, :], modT[:, 24:32, b]), [r.mod2], [r.modv[b][4]], "m7")
                dve(lambda e, b=b: e.tensor_copy(modv[:, b, 5, :], modT[:, 40:48, b]), [r.mod2], [r.modv[b][5]], "m8")

        def build_gate_bc(b, kind, dst, dres):
            for c in range(8):
                tb = c % 2
                dve(lambda e, c=c, tb=tb: e.tensor_copy(tmpf[tb][:, :], modv[:, b, kind, c:c + 1].to_broadcast([128, 128])),
                    [r.modv[b][kind]], [r.tmpf[tb]], "gbc0")
                pe(lambda e, tb=tb: e.transpose(bank(1 + tb)[:, 0:128], tmpf[tb][:, :], ident_f[:, :]),
                   [r.tmpf[tb], r.const], [pb[1 + tb]], "gbcT")
                dve(lambda e, c=c, tb=tb: e.tensor_copy(dst[:, c * 128:(c + 1) * 128], bank(1 + tb)[:, 0:128]),
                    [pb[1 + tb]], [dres], "gbc1")

        ckpt("P0")
        out_dmas = []
        for s in range(SEQ_PER_CORE):
            if s > 0:
                P.barrier()
            win_v = win_d.rearrange("(k p) f -> p k f", p=128)
            dma("pool", win_sb[:, :, 0:512], win_v[:, :, 0:512], [], [r.winb["qa"]] + r.wada)
            dma("pool", win_sb[:, :, 512:1024], win_v[:, :, 512:1024], [], [r.winb["ka"]] + r.wada)
            def fones(e):
                e.memset(vna[:, :, :, 64:65], 1.0)
                return e.memset(vsw[:, :, :, 64:65], 1.0)
            pool(fones, [], r.vna + r.vsw, "ones")

            def fzero(e):
                kv = knam[:, :, :].rearrange("p (a two) t -> p a two t", two=2)
                e.memset(kv[64:128, :, 0, :], 0.0)
                e.memset(kv[0:64, :, 1, :], 0.0)
                sv = ktswm[:, :, :].rearrange("p (a two) t -> p a two t", two=2)
                e.memset(sv[64:128, :, 0, :], 0.0)
                return e.memset(sv[0:64, :, 1, :], 0.0)
            pool(fzero, [], [x for l in r.knam for x in l] + [x for l in r.ktswm for x in l], "kzero")
            dma("pool", win_sb[:, :, 1024:1536], win_v[:, :, 1024:1536], [], [r.winb["va"]] + r.wada)
            dma("pool", win_sb[:, :, 2304:2432], win_v[:, :, 2176:2304], [], [r.winb["vb"]] + r.wada)
            dma("pool", win_sb[:, :, 1536:2048], win_v[:, :, 1536:2048], [], [r.winb["qb"]] + r.wada)
            dma("pool", win_sb[:, :, 2048:2112], win_v[:, :, 2048:2112], [], [r.winb["kb0"]] + r.wada)
            dma("pool", win_sb[:, :, 2112:2176], win_v[:, :, 2048:2112], [], [r.winb["kb1"]] + r.wada)
            dma("pool", win_sb[:, :, 2176:2240], win_v[:, :, 2112:2176], [], [r.winb["kb2"]] + r.wada)
            dma("pool", win_sb[:, :, 2240:2304], win_v[:, :, 2112:2176], [], [r.winb["kb3"]] + r.wada)

            p1st = {}

            def p1_load(t):
                dma("sp", xin[t % 2][:, :], x_d[s, t * 128:(t + 1) * 128, :], [], [r.xin[t % 2]])

            def p1_iter(it):
                def ap(t, part):
                    xb = t % 2
                    st_, sres_ = p1st[t]
                    rms_apply(xin[xb][:, :], r.xin[xb], st_, sres_, xn[xb], r.xn[xb], 6 + xb, hT, r.hT2[t], t,
                              lambda c, s=s: modv[:, s, 0, c:c + 1], lambda c, s=s: modv[:, s, 1, c:c + 1], r.modv[s][0:2],
                              evac_dve=True, part=part)
                if it < NT:
                    t, xb = it, it % 2
                    p1st[t] = rms_stats(xin[xb][:, :], r.xin[xb], xn[xb], r.xn[xb])
                if it >= 1:
                    ap(it - 1, "rest")
                if it < NT:
                    ap(it, "xn")
                if it + 1 < NT:
                    p1_load(it + 1)

            fm_plain, fm_rope = [], []
            for cc in range(4):
                fm_plain.append(("qa", qna, cc, r.qna[cc], cc * 128))
            for cc in range(4):
                fm_plain.append(("ka", knam, cc, None, 512 + cc * 128))
            for cc in range(4):
                fm_rope.append(("qb", qtsw, cc, r.qtsw[cc], 1536 + cc * 128))
            for cc in range(2):
                fm_rope.append(("kb", ktswm, cc, None, 2048 + cc * 128))
            pcnt = [0]
            rcnt = [0]
            pending = []
            first_rope = [True]

            def p2_group(kind, dst, dc_, dres, col, b4):
                pbk = pcnt[0] % 3
                pcnt[0] += 1

                def fmm(e):
                    ins = None
                    for k in range(8):
                        ins = e.matmul(bank(pbk), win_sb[:, k, col:col + 128], hT[:, k, b4 * 512:(b4 + 1) * 512],
                                       start=(k == 0), stop=(k == 7))
                    return ins
                wres = [r.winb[kind]] if kind != "kb" else [r.winb["kb%d" % i] for i in range(4)]
                pe(fmm, wres + r.hT[8 * b4:8 * b4 + 8], [pb[pbk]], "proj")
                bsl = slice(b4 * 512, (b4 + 1) * 512)
                dsl = dst[:, dc_, bsl]
                if kind == "qa":
                    act(lambda e: e.activation(out=dsl, in_=bank(pbk), func=AF.Copy, scale=0.125),
                        [pb[pbk]], [dres[b4]], "qa")
                elif kind == "ka":
                    def fka(e):
                        e.activation(out=knam[0:64, 2 * dc_, bsl], in_=bank(pbk)[0:64, :], func=AF.Copy)
                        return e.activation(out=knam[64:128, 2 * dc_ + 1, bsl], in_=bank(pbk)[64:128, :], func=AF.Copy)
                    act(fka, [pb[pbk]], [r.knam[2 * dc_][b4], r.knam[2 * dc_ + 1][b4]], "ka")
                else:
                    rb = rcnt[0] % 2
                    rcnt[0] += 1
                    sc = 0.125 if kind == "qb" else 1.0
                    extra = [r.xin[0], r.xin[1], r.xn[0], r.xn[1]] + r.wada2 if first_rope[0] else []
                    first_rope[0] = False
                    act(lambda e: e.activation(out=qf[rb][:, :], in_=bank(pbk), func=AF.Copy, scale=sc),
                        [pb[pbk]], [r.qf[rb]] + extra, "qf")

                    def post():
                        pe(lambda e: e.matmul(bank(3 + rb), perm_f[:, :], qf[rb][:, :], start=True, stop=True),
                           [r.qf[rb], r.const], [pb[3 + rb]], "perm")
                        pool(lambda e: e.tensor_tensor(ra[rb][:, :], qf[rb][:, :], ropec[:, bsl], ALU.mult),
                             [r.qf[rb], r.ropeC], [r.ra[rb]], "ropeA")
                        dve(lambda e: e.tensor_tensor(qf[rb][:, :], bank(3 + rb), ropes[:, bsl], ALU.mult),
                            [pb[3 + rb], r.ropeS], [r.qf[rb]], "ropeB")
                        if kind == "qb":
                            dve(lambda e: e.tensor_tensor(dsl, qf[rb][:, :], ra[rb][:, :], ALU.add),
                                [r.ra[rb], r.qf[rb]], [dres[b4]], "ropeC")
                        else:
                            def fkb(e):
                                e.tensor_tensor(ktswm[0:64, 2 * dc_, bsl], qf[rb][0:64, :], ra[rb][0:64, :], ALU.add)
                                return e.tensor_tensor(ktswm[64:128, 2 * dc_ + 1, bsl], qf[rb][64:128, :], ra[rb][64:128, :], ALU.add)
                            dve(fkb, [r.ra[rb], r.qf[rb]], [r.ktswm[2 * dc_][b4], r.ktswm[2 * dc_ + 1][b4]], "ropeC")
                    pending.append(post)
                    if len(pending) > 1:
                        pending.pop(0)()

            def p2_v(t):
                hv = (t % 2) * 128

                def fv(e):
                    ins = None
                    for k in range(8):
                        ins = e.matmul(bank(5), hT[:, k, t * 128:(t + 1) * 128], win_sb[:, k, 1024:1536],
                                       start=(k == 0), stop=(k == 7))
                    for k in range(8):
                        ins = e.matmul(bank(4)[:, hv:hv + 128], hT[:, k, t * 128:(t + 1) * 128],
                                       win_sb[:, k, 2304:2432], start=(k == 0), stop=(k == 7))
                    return ins
                pe(fv, [r.winb["va"], r.winb["vb"]] + r.hT2[t], [pb[5], pb[4]], "vproj")
                dve(lambda e: e.tensor_copy(vna[:, t, :, 0:64], bank(5).rearrange("p (h d) -> p h d", d=64)),
                    [pb[5]], [r.vna[t]], "vna")
                dve(lambda e: e.tensor_copy(vsw[:, t, :, 0:64], bank(4)[:, hv:hv + 128].rearrange("p (h d) -> p h d", d=64)),
                    [pb[4]], [r.vsw[t]], "vsw")

            modq = list(range(4, 12)) if s == 0 else []
            p1_load(0)
            for it in range(5):
                p1_iter(it)
            for b4 in range(4):
                if b4 == 1:
                    dma("sp", ropec[:, :], ropec_d, [], [r.ropeC])
                    dma("sp", ropes[:, :], ropes_d, [], [r.ropeS])
                slots = [lambda ch=ch, b4=b4: p2_group(*ch, b4) for ch in fm_plain] + \
                        [lambda t=t: p2_v(t) for t in range(4 * b4, 4 * b4 + 4)]
                nxt = [it for it in range(4 * b4 + 5, 4 * b4 + 9) if it <= NT]
                for si, slot in enumerate(slots):
                    slot()
                    if si % 3 == 2 and nxt:
                        p1_iter(nxt.pop(0))
                    if s == 0 and b4 >= 2 and si % 3 == 1 and modq:
                        mod_tile(modq.pop(0), wada2_sb, r.wada2, 3)
                while nxt:
                    p1_iter(nxt.pop(0))
            while s == 0 and modq:
                mod_tile(modq.pop(0), wada2_sb, r.wada2, 3)
            if s == 0:
                mod_rest_finish()
            ckpt("P1")
            for ch in fm_rope:
                for b4 in range(4):
                    p2_group(*ch, b4)
            while pending:
                pending.pop(0)()

            ckpt("P2")
            P.barrier()
            for h_ in range(8):
                dma("pool", nab[:, h_ * N_NATBL:(h_ + 1) * N_NATBL, :], nab_d[:, h_ * N_NATBL:(h_ + 1) * N_NATBL, :], [], [r.nabh[h_]])
            build_gate_bc(s, 2, gatea_bc, r.gatea)

            def issue_wout_fold(c):
                wb = c % 2
                dma("sp", wstg[wb][:, :], wout_d[c * 128:(c + 1) * 128, :], [], [r.wstg[wb]])
                dve(lambda e: e.scalar_tensor_tensor(woutg[:, c, :], wstg[wb][:, :], paramsT[:, 64 + c:65 + c],
                                                     gatea_bc[:, :], ALU.mult, ALU.mult),
                    [r.wstg[wb], r.params, r.gatea], [r.woutg], "woutg")

            def oacc_view(b0):
                return [ps[:, 512 * (b0 + k):512 * (b0 + k + 1)].rearrange("p (h d) -> p h d", d=128) for k in range(2)]

            def attn_finish(ob, tile, half, with_sink, cnt, defer_b=False):
                ov = oacc_view(ob)
                rd = cnt % 2
                rdv = [rden_t[:, rd, 4 * k:4 * k + 4] for k in range(2)]
                if with_sink:
                    dtv = [dtmp_t[:, rd, 4 * k:4 * k + 4] for k in range(2)]

                    def fs(e):
                        ins = None
                        for k in range(2):
                            ins = e.tensor_tensor(dtv[k], ov[k][:, :, 64], esink[:, 4 * k:4 * k + 4], ALU.add)
                        return ins
                    dve(fs, [pb[ob], pb[ob + 1], r.const], [r.dtmp[rd]], "dsink")

                    def fr(e):
                        ins = None
                        for k in range(2):
                            ins = e.reciprocal(rdv[k], dtv[k])
                        return ins
                    dve(fr, [r.dtmp[rd]], [r.rden[rd]], "rden")
                else:
                    def fr(e):
                        ins = None
                        for k in range(2):
                            ins = e.reciprocal(rdv[k], ov[k][:, :, 64])
                        return ins
                    dve(fr, [pb[ob], pb[ob + 1]], [r.rden[rd]], "rden")
                ob_ = osb[rd]

                def fn_(e):
                    ins = None
                    for k in range(2):
                        ins = e.tensor_tensor(ob_[:, 256 * k:256 * (k + 1)].rearrange("p (h d) -> p h d", h=4), ov[k][:, :, 0:64],
                                              rdv[k].unsqueeze(2).broadcast_to([128, 4, 64]), ALU.mult)
                    return ins
                dve(fn_, [pb[ob], pb[ob + 1], r.rden[rd]], [r.osb[rd]], "onorm")

                def part_b():
                    st, sres = new_stat()
                    dsl = cat[:, tile, half * 512:(half + 1) * 512]
                    act(lambda e: e.activation(out=dsl, in_=ob_[:, :], func=AF.Square, accum_out=st[:, 0:1]),
                        [r.osb[rd]], [sres, r.cat[tile][half]], "ossq")
                    rstd_from_ssq(st, sres, 512.0)
                    dve(lambda e: e.tensor_scalar(dsl, ob_[:, :], st[:, 1:2], None, ALU.mult),
                        [r.osb[rd], sres], [r.cat[tile][half]], "ocat")
                if defer_b:
                    return part_b
                part_b()
                return None

            items = [(j, h) for j in range(NT) for h in range(8)]

            def na_qk(idx):
                j, h = items[idx]
                lo, hi = na_tiles(j)
                sb_ = idx % 2
                cc, hp = h // 2, h % 2

                def f(e):
                    ins = None
                    for i in range(lo, hi + 1):
                        o = ps[:, 1024 * sb_ + (i - lo) * 128:1024 * sb_ + (i - lo + 1) * 128]
                        e.matmul(o, knam[:, h, i * 128:(i + 1) * 128], qna[:, cc, j * 128:(j + 1) * 128],
                                 start=True, stop=False)
                        ins = e.matmul(o, ident_bf[:, :], nab[:, h * N_NATBL + NA_TBL_OF[(j, i)], :], start=False, stop=True)
                    return ins
                rr = [r.qna[cc][j // 4], r.nabh[h], r.const] + [r.knam[h][i // 4] for i in range(lo, hi + 1)]
                pe(f, rr, [pb[2 * sb_], pb[2 * sb_ + 1]], "naqk")

            def na_exp(idx):
                j, h = items[idx]
                lo, hi = na_tiles(j)
                n = hi - lo + 1
                sb_ = idx % 2
                pt = idx % 3
                def fx(e):
                    n1 = min(n, 4)
                    ins = e.activation(out=ptna[pt][:, 0:n1 * 128], in_=ps[:, 1024 * sb_:1024 * sb_ + n1 * 128], func=AF.Exp)
                    if n > 4:
                        ins = e.activation(out=ptna[pt][:, 512:640], in_=ps[:, 1024 * sb_ + 512:1024 * sb_ + 640], func=AF.Exp)
                    return ins
                act(fx, [pb[2 * sb_], pb[2 * sb_ + 1]], [r.ptna[pt]], "naexp")

            def na_pv(idx):
                j, h = items[idx]
                lo, hi = na_tiles(j)
                pt = idx % 3
                ob = 4 + 2 * (j % 2)
                ov = oacc_view(ob)

                def f(e):
                    ins = None
                    for i in range(lo, hi + 1):
                        ins = e.matmul(ov[h // 4][:, h % 4, 0:65], ptna[pt][:, (i - lo) * 128:(i - lo + 1) * 128],
                                       vna[:, i, h, :], start=(i == lo), stop=(i == hi))
                    return ins
                pe(f, [r.ptna[pt]] + [r.vna[i] for i in range(lo, hi + 1)], [pb[ob + h // 4]], "napv")
                if h == 7:
                    fin_q.append((idx + 2, lambda: attn_finish(ob, j, 0, False, j)))

            fin_q = []
            na_qk(0)
            for idx in range(len(items)):
                na_exp(idx)
                while fin_q and fin_q[0][0] <= idx:
                    fin_q.pop(0)[1]()
                if idx + 1 < len(items):
                    na_qk(idx + 1)
                na_pv(idx)
                if idx % 8 == 3 and idx // 8 < 8:
                    issue_wout_fold(idx // 8)
            while fin_q:
                fin_q.pop(0)[1]()

            ckpt("P3")
            P.barrier()
            sitems = [(n, g) for n in range(NT) for g in range(2)]

            def sw_deltas(n):
                return [d for d in (-1, 0, 1) if 0 <= n + d < NT]

            def sw_qk(idx):
                n, g = sitems[idx]
                sb_ = idx % 2
                base = 1536 * sb_

                def f(e):
                    ins = None
                    for d in sw_deltas(n):
                        di = d + 1
                        kb = n + d
                        for hh in range(4):
                            hp = hh % 2
                            o = ps[:, base + di * 512 + hh * 128:base + di * 512 + (hh + 1) * 128]
                            ins = e.matmul(o, ktswm[:, 2 * g + hp, kb * 128:(kb + 1) * 128],
                                           qtsw[:, 2 * g + hh // 2, n * 128:(n + 1) * 128], start=True, stop=(d == 0))
                            if d != 0:
                                mi = 0 if d == -1 else 1
                                ins = e.matmul(o, ident_bf[:, :], swm[:, mi, :], start=False, stop=True)
                    return ins
                rr = [r.const, r.qtsw[2 * g][n // 4], r.qtsw[2 * g + 1][n // 4]] + [r.ktswm[2 * g + hp_][(n + d) // 4] for d in sw_deltas(n) for hp_ in range(2)]
                pe(f, rr, [pb[3 * sb_], pb[3 * sb_ + 1], pb[3 * sb_ + 2]], "swqk")

            def sw_exp(idx):
                n, g = sitems[idx]
                sb_ = idx % 2
                base = 1536 * sb_
                ds_ = sw_deltas(n)
                d0, d1 = ds_[0] + 1, ds_[-1] + 1
                def fx(e):
                    ins = None
                    for di in range(d0, d1 + 1):
                        ins = e.activation(out=ptsw[sb_][:, di, :], in_=ps[:, base + di * 512:base + (di + 1) * 512], func=AF.Exp)
                    return ins
                act(fx, [pb[3 * sb_], pb[3 * sb_ + 1], pb[3 * sb_ + 2]], [r.ptsw[sb_]], "swexp")

            def sw_pv(idx):
                n, g = sitems[idx]
                sb_ = idx % 2
                ov = oacc_view(6)
                ds_ = sw_deltas(n)

                def f(e):
                    ins = None
                    for hh in range(4):
                        for d in ds_:
                            ins = e.matmul(ov[g][:, hh, 0:65], ptsw[sb_][:, d + 1, hh * 128:(hh + 1) * 128],
                                           vsw[:, n + d, g, :], start=(d == ds_[0]), stop=(d == ds_[-1]))
                    return ins
                pe(f, [r.ptsw[sb_]] + [r.vsw[n + d] for d in ds_], [pb[6 + g]], "swpv")
                if g == 1:
                    fin_sw.append((idx + 2, attn_finish(6, n, 1, True, n, defer_b=True)))

            fin_sw = []
            sw_qk(0)
            for idx in range(len(sitems)):
                sw_exp(idx)
                while fin_sw and fin_sw[0][0] <= idx:
                    fin_sw.pop(0)[1]()
                if idx + 1 < len(sitems):
                    sw_qk(idx + 1)
                sw_pv(idx)
            while fin_sw:
                fin_sw.pop(0)[1]()

            ckpt("P4")
            P.barrier()
            build_gate_bc(s, 5, gatef_bc, r.gatef)
            p5st = {}

            def p5_s1(t):
                tb = t % 2
                pbf = bank_bf(tb)
                dma("sp", xin5[tb][:, :], x_d[s, t * 128:(t + 1) * 128, :], [], [r.xin[tb]])

                def ft(e):
                    ins = None
                    for c in range(8):
                        ins = e.transpose(pbf[:, c * 128:(c + 1) * 128], cat[:, t, c * 128:(c + 1) * 128], ident_bf[:, :])
                    return ins
                pe(ft, [r.cat[t][0], r.cat[t][1], r.const], [pb[tb]], "catT")
                act(lambda e: e.activation(out=cT[tb][:, :, :].rearrange("p c q -> p (c q)"), in_=pbf, func=AF.Copy),
                    [pb[tb]], [r.cT[tb]], "cT")

            def p5_s2(t):
                tb = t % 2
                mb = 2 + 2 * tb

                def fo(e):
                    ins = None
                    for hf in range(2):
                        for c in range(8):
                            ins = e.matmul(bank(mb + hf), cT[tb][:, c, :], woutg[:, c, hf * 512:(hf + 1) * 512],
                                           start=(c == 0), stop=(c == 7))
                    return ins
                pe(fo, [r.cT[tb], r.woutg], [pb[mb], pb[mb + 1]], "outproj")

                def fx1(e):
                    ins = None
                    for hf in range(2):
                        ins = e.tensor_tensor(x1[:, t, hf * 512:(hf + 1) * 512], xin5[tb][:, hf * 512:(hf + 1) * 512],
                                              bank(mb + hf), ALU.add)
                    return ins
                dve(fx1, [r.xin[tb], pb[mb], pb[mb + 1]], [r.x1[t]], "x1")
                p5st[t] = rms_stats(x1[:, t, :], r.x1[t], xn5[tb], r.xn[tb])

            def p5_s3(t, part, s=s):
                tb = t % 2
                st_, sres_ = p5st[t]
                rms_apply(x1[:, t, :], r.x1[t], st_, sres_, xn5[tb], r.xn[tb], 6 + tb, h2T, r.h2T2[t], t,
                          lambda c: modv[:, s, 3, c:c + 1], lambda c: modv[:, s, 4, c:c + 1], r.modv[s][3:5],
                          evac_dve=True, part=part)

            for it in range(NT + 2):
                if 0 <= it - 2 < NT:
                    p5_s3(it - 2, "xn")
                if it < NT:
                    p5_s1(it)
                if 0 <= it - 1 < NT:
                    p5_s2(it - 1)
                if 0 <= it - 2 < NT:
                    p5_s3(it - 2, "rest")

            ckpt("P5")
            P.barrier()
            dma("sp", gfin_bc[:, :], gfin_d.partition_broadcast(128), [], [r.gfin])
            if True:
                def fz(e):
                    for g_ in Gb:
                        e.memset(g_[:, 0:1], 0.0)
                        ins = e.memset(g_[:, 2049:2050], 0.0)
                    return ins
                pool(fz, [], r.G, "gzero")
            wup_v = wup_d.rearrange("(k p) f -> p k f", p=128)
            dcnt = [0]
            group_of = {}
            for gi_, (c0_, c1_) in enumerate(GROUPS):
                for c_ in range(c0_, c1_):
                    group_of[c_] = gi_

            def wup_load(c):
                wb = c % 3
                dma("pool", wup_sb[wb][:, :, 128:256], wup_v[:, :, DFF + c * 128:DFF + (c + 1) * 128], [], [r.wupg[wb]])
                dma("pool", wup_sb[wb][:, :, 0:128], wup_v[:, :, c * 128:(c + 1) * 128], [], [r.wup[wb]])

            def issue_gate(c):
                wb, gb = c % 3, c % 2
                for b4 in range(4):
                    gk = b4 % 2

                    def fg(e, wb=wb, b4=b4, gk=gk):
                        ins = None
                        for k in range(8):
                            ins = e.matmul(bank(gk), wup_sb[wb][:, k, 128:256], h2T[:, k, b4 * 512:(b4 + 1) * 512],
                                           start=(k == 0), stop=(k == 7))
                        return ins
                    pe(fg, [r.wupg[wb]] + r.h2T[8 * b4:8 * b4 + 8], [pb[gk]], "gate")
                    act(lambda e, gb=gb, b4=b4, gk=gk: e.activation(out=Gb[gb][:, 1 + b4 * 512:1 + (b4 + 1) * 512],
                                                                   in_=bank(gk), func=AF.Copy),
                        [pb[gk]], [r.G[gb]], "gevac")

            def issue_conv(c):
                gb = c % 2
                t1, t1r = T1[c % 2], r.T1[c % 2]
                cw0 = paramsT[:, 94 + c:95 + c]
                cw1 = paramsT[:, 94 + 22 + c:95 + 22 + c]
                cw2 = paramsT[:, 94 + 44 + c:95 + 44 + c]
                cbb = paramsT[:, 72 + c:73 + c]
                pool(lambda e: e.tensor_scalar(t1[:, :], Gb[gb][:, 0:2048], cw0, cbb, ALU.mult, ALU.add),
                     [r.G[gb], r.params], [t1r], "conv0")
                dve(lambda e: e.scalar_tensor_tensor(t1[:, :], Gb[gb][:, 1:2049], cw1, t1[:, :], ALU.mult, ALU.add),
                    [r.G[gb], r.params, t1r], [t1r], "conv1")
                dve(lambda e: e.scalar_tensor_tensor(t1[:, :], Gb[gb][:, 2:2050], cw2, t1[:, :], ALU.mult, ALU.add),
                    [r.G[gb], r.params, t1r], [t1r], "conv2")

            def issue_silu(c):
                t1, t1r = T1[c % 2], r.T1[c % 2]
                act(lambda e: e.activation(out=t1[:, :], in_=t1[:, :], func=AF.Silu), [t1r], [t1r], "silu")

            def issue_val(c):
                wb = c % 3
                t1, t1r = T1[c % 2], r.T1[c % 2]
                ci = c - GROUPS[group_of[c]][0]
                for b4 in range(4):
                    vk = 2 + b4 % 2

                    def fvv(e, wb=wb, b4=b4, vk=vk):
                        ins = None
                        for k in range(8):
                            ins = e.matmul(bank(vk), wup_sb[wb][:, k, 0:128], h2T[:, k, b4 * 512:(b4 + 1) * 512],
                                           start=(k == 0), stop=(k == 7))
                        return ins
                    pe(fvv, [r.wup[wb]] + r.h2T[8 * b4:8 * b4 + 8], [pb[vk]], "val")
                    dve(lambda e, ci=ci, b4=b4, vk=vk: e.tensor_tensor(uT[:, ci, b4 * 512:(b4 + 1) * 512],
                                                                      t1[:, b4 * 512:(b4 + 1) * 512], bank(vk), ALU.mult),
                        [t1r, pb[vk]], [r.uT[ci][b4]], "umul")

            def issue_fold(c):
                ci = c - GROUPS[group_of[c]][0]
                wb = c % 2
                dma("sp", wdstg[wb][:, :], wdn_d[c * 128:(c + 1) * 128, :], [], [r.wdstg[wb]])
                dve(lambda e: e.tensor_tensor(wdng[:, ci, :], wdstg[wb][:, :], gatef_bc[:, :], ALU.mult),
                    [r.wdstg[wb], r.gatef], [r.wdng[ci]], "wdng")

            fin6 = []

            def issue_down(gi):
                c0, c1 = GROUPS[gi]
                last = gi == len(GROUPS) - 1
                ng = c1 - c0
                for t in range(NT):
                    for hf in range(2):
                        dk = 4 + dcnt[0] % 4
                        dcnt[0] += 1

                        def fd(e, t=t, hf=hf, dk=dk, ng=ng):
                            ins = None
                            for ci in range(ng):
                                ins = e.matmul(bank(dk), uT[:, ci, t * 128:(t + 1) * 128], wdng[:, ci, hf * 512:(hf + 1) * 512],
                                               start=(ci == 0), stop=(ci == ng - 1))
                            return ins
                        pe(fd, [r.uT[ci][t // 4] for ci in range(ng)] + r.wdng[0:ng], [pb[dk]], "down")
                        dve(lambda e, t=t, hf=hf, dk=dk: e.tensor_tensor(x1[:, t, hf * 512:(hf + 1) * 512],
                                                                        x1[:, t, hf * 512:(hf + 1) * 512], bank(dk), ALU.add),
                            [pb[dk], r.x1[t]], [r.x1[t]], "x2acc")
                    if last:
                        def fstats(t=t):
                            ob = t % 2
                            st, sres = new_stat()
                            act(lambda e: e.activation(out=ostg[ob][:, :], in_=x1[:, t, :], func=AF.Square, accum_out=st[:, 0:1]),
                                [r.x1[t]], [sres, r.ostg[ob]], "fssq")
                            rstd_from_ssq(st, sres, 1024.0)
                            return st, sres

                        def ffinal(stp, t=t):
                            ob = t % 2
                            st, sres = stp
                            dve(lambda e: e.scalar_tensor_tensor(ostg[ob][:, :], x1[:, t, :], st[:, 1:2],
                                                                 gfin_bc[:, :], ALU.mult, ALU.mult),
                                [r.x1[t], sres, r.gfin], [r.ostg[ob]], "final")
                            out_dmas.append(dma("sp", out_d[s, t * 128:(t + 1) * 128, :], ostg[ob][:, :], [r.ostg[ob]], []))
                        fin6.append([t, fstats, ffinal, None])
                        for ent in fin6:
                            if ent[3] is None and ent[0] <= t - 1:
                                ent[3] = ent[1]()
                        while fin6 and fin6[0][0] <= t - 2:
                            ent = fin6.pop(0)
                            ent[2](ent[3])
                if last:
                    for ent in fin6:
                        if ent[3] is None:
                            ent[3] = ent[1]()
                    while fin6:
                        ent = fin6.pop(0)
                        ent[2](ent[3])

            wup_load(0)
            wup_load(1)
            wup_load(2)
            for c in range(NCH):
                if c >= 2 and c + 1 < NCH:
                    wup_load(c + 1)
                issue_gate(c)
                if c > 0:
                    issue_silu(c - 1)
                    issue_val(c - 1)
                issue_conv(c)
                if c > 0 and group_of[c - 1] != group_of[c]:
                    issue_down(group_of[c - 1])
                issue_fold(c)
            issue_silu(NCH - 1)
            issue_val(NCH - 1)
            issue_down(len(GROUPS) - 1)

    except _Stop:
        pass

    P.barrier()

    with ExitStack() as es:
        es.enter_context(nc.allow_low_precision("bf16 matmul operands, fp32 accumulation"))
        es.enter_context(nc.allow_non_contiguous_dma("small strided parameter loads"))
        eng_sems = {e: es.enter_context(nc.semaphore(f"sem_{e}")) for e in ENGS}
        dma_sems = {q: [es.enter_context(nc.semaphore(f"dsem_{q}{i}")) for i in range(n)]
                    for q, n in P.n_dma_sems.items()}
        block = es.enter_context(nc.Block())
        P.emit(nc, block, eng_sems, dma_sems)
    return nc


_CONSTS = {}


def _consts():
    if not _CONSTS:
        C, S = build_rope()
        _CONSTS.update(dict(rope_c=C, rope_s=S, swmask=build_sw_mask(),
                            ident=np.eye(128, dtype=np.float32), perm=build_perm()))
    return _CONSTS


def make_in_maps(x, c, w_ada, b_ada, g_attn, w_in, na_rpb, sw_sink, g_na_out, g_sw_out,
                 w_out, g_ffn, w_up, conv_w, conv_b, w_down, g_final, cores=range(N_CORES)):
    f = lambda a: np.ascontiguousarray(np.asarray(a, dtype=np.float32))
    x = f(x); c = f(c)
    cst = _consts()
    nabias = build_na_bias(f(na_rpb)[0])
    base_rows = [f(b_ada)[0].reshape(48, 128), f(g_attn)[0].reshape(8, 128), f(g_ffn)[0].reshape(8, 128),
                 f(g_na_out)[0].reshape(4, 128), f(g_sw_out)[0].reshape(4, 128),
                 f(conv_b)[0].reshape(22, 128), f(conv_w)[0].reshape(66, 128)]
    shared = dict(w_ada=f(w_ada)[0], w_in=f(w_in)[0], w_out=f(w_out)[0], w_up=f(w_up)[0], w_down=f(w_down)[0],
                  g_final=f(g_final), sw_sink=f(sw_sink)[0], nabias=nabias, **cst)
    in_maps = []
    for core in cores:
        b0 = core * SEQ_PER_CORE
        rows = base_rows + [c[b0].reshape(8, 128), c[b0 + 1].reshape(8, 128)]
        params = np.zeros((256, 128), np.float32)
        cat_rows = np.concatenate(rows, axis=0)
        params[:cat_rows.shape[0]] = cat_rows
        m = dict(shared)
        m["x"] = np.ascontiguousarray(x[b0:b0 + SEQ_PER_CORE])
        m["params"] = params
        in_maps.append(m)
    return in_maps


def kernel(x, c, w_ada, b_ada, g_attn, w_in, na_rpb, sw_sink, g_na_out, g_sw_out,
           w_out, g_ffn, w_up, conv_w, conv_b, w_down, g_final):
    in_maps = make_in_maps(x, c, w_ada, b_ada, g_attn, w_in, na_rpb, sw_sink, g_na_out, g_sw_out,
                           w_out, g_ffn, w_up, conv_w, conv_b, w_down, g_final)
    nc = build_program()
    res = run_bass_kernel_spmd(nc, in_maps, core_ids=list(range(N_CORES)))
    out = np.concatenate([np.asarray(r["out"]) for r in res.results], axis=0)
    return out.astype(np.float32)
```

```python
import numpy as np
from contextlib import ExitStack

import concourse.bass as bass
import concourse.mybir as mybir
from concourse.bass_utils import run_bass_kernel_spmd

F32 = mybir.dt.float32
BF16 = mybir.dt.bfloat16
AF = mybir.ActivationFunctionType
ALU = mybir.AluOpType

D = 1024
SEQ = 2048
NT = 16
DFF = 2816
NCH = 22
EPS = 1e-6
NEG = -30000.0
N_CORES = 8
SEQ_PER_CORE = 2
GROUPS = [(0, 6), (6, 12), (12, 17), (17, 22)]


class Res:
    __slots__ = ("name", "w", "rd", "rdma")

    def __init__(self, name):
        self.name = name
        self.w = None
        self.rd = {}
        self.rdma = []


class Op:
    __slots__ = ("eng", "fn", "deps", "dma", "sem", "semval", "signals", "sigidx", "name")


ENGS = ["pe", "act", "dve", "pool", "sp"]


class Prog:
    def __init__(self, n_dma_sems):
        self.ops = []
        self.n_dma_sems = n_dma_sems
        self.dma_last = {q: [None] * n for q, n in n_dma_sems.items()}
        self.dma_cnt = {q: [0] * n for q, n in n_dma_sems.items()}
        self.dma_rr = {q: 0 for q in n_dma_sems}
        self.last_op = {e: None for e in ENGS}
        self.pending_dma = []

    def add(self, eng, fn, reads=(), writes=(), dma=False, name="", extra_deps=()):
        op = Op()
        op.eng, op.fn, op.dma, op.name = eng, fn, dma, name
        op.signals = False
        op.sigidx = 0
        op.sem = None
        op.semval = 0
        deps = list(extra_deps)
        for r in reads:
            if r.w is not None:
                deps.append(r.w)
        for w in writes:
            if w.w is not None:
                deps.append(w.w)
            deps.extend(w.rd.values())
            deps.extend(w.rdma)
        if dma:
            q = eng
            n = self.n_dma_sems[q]
            j = self.dma_rr[q]
            self.dma_rr[q] = (j + 1) % n
            prev = self.dma_last[q][j]
            if prev is not None:
                deps.append(prev)
            self.dma_cnt[q][j] += 1
            op.sem = (q, j)
            op.semval = 16 * self.dma_cnt[q][j]
            self.dma_last[q][j] = op
            self.pending_dma.append(op)
        for r in reads:
            if dma:
                r.rdma.append(op)
            else:
                r.rd[eng] = op
        for w in writes:
            w.w = op
            w.rd = {}
            w.rdma = []
        seen = set()
        dl = []
        for d in deps:
            if d is op or d is None or id(d) in seen:
                continue
            seen.add(id(d))
            dl.append(d)
        op.deps = dl
        self.ops.append(op)
        self.last_op[eng] = op
        return op

    def barrier(self):
        lasts = dict(self.last_op)
        pend = list(self.pending_dma)
        self.pending_dma = []
        for e in ENGS:
            deps = [o for ee, o in lasts.items() if ee != e and o is not None] + pend
            self.add(e, None, extra_deps=deps, name="barrier")

    def emit(self, nc, block, eng_sems, dma_sems):
        for op in self.ops:
            for d in op.deps:
                if not d.dma and (d.eng != op.eng or op.dma or d.eng != "pe"):
                    d.signals = True
        cnt = {e: 0 for e in ENGS}
        for op in self.ops:
            if (not op.dma) and op.signals:
                if op.fn is None:
                    op.signals = False
                    op.sigidx = cnt[op.eng]
                else:
                    cnt[op.eng] += 1
                    op.sigidx = cnt[op.eng]
        by_eng = {e: [o for o in self.ops if o.eng == e] for e in ENGS}

        def run(e, eng):
            waited = {}
            for op in by_eng[e]:
                need = {}
                for d in op.deps:
                    if d.dma:
                        key = ("dma",) + d.sem
                        val = d.semval
                    else:
                        if d.eng == op.eng and d.eng == "pe" and not op.dma:
                            continue
                        key = ("eng", d.eng)
                        val = d.sigidx
                    if val > need.get(key, 0):
                        need[key] = val
                for key, val in need.items():
                    if waited.get(key, 0) < val:
                        sem = eng_sems[key[1]] if key[0] == "eng" else dma_sems[key[1]][key[2]]
                        eng.wait_ge(sem, val)
                        waited[key] = val
                if op.fn is None:
                    continue
                ins = op.fn(eng)
                if op.dma:
                    ins.then_inc(dma_sems[op.sem[0]][op.sem[1]], 16)
                elif op.signals:
                    ins.then_inc(eng_sems[e], 1)

        @block.tensor
        def _(eng):
            run("pe", eng)

        @block.scalar
        def _(eng):
            run("act", eng)

        @block.vector
        def _(eng):
            run("dve", eng)

        @block.gpsimd
        def _(eng):
            run("pool", eng)

        @block.sync
        def _(eng):
            run("sp", eng)


def na_tiles(j):
    lo = min(max(j - 2, 0), 12)
    hi = max(min(j + 2, 15), 3)
    return lo, hi


def na_table_struct():
    sigs = {}
    tbl_of = {}
    for j in range(16):
        lo, hi = na_tiles(j)
        for i in range(lo, hi + 1):
            sig = []
            for qr in (2 * j, 2 * j + 1):
                rs = min(max(qr - 4, 0), 24)
                for kr in (2 * i, 2 * i + 1):
                    sig.append((kr - qr) if (rs <= kr < rs + 8) else None)
            sig = tuple(sig)
            if sig not in sigs:
                sigs[sig] = len(sigs)
            tbl_of[(j, i)] = sigs[sig]
    return tbl_of, sigs


NA_TBL_OF, NA_SIGS = na_table_struct()
N_NATBL = len(NA_SIGS)


def build_na_bias(rpb):
    kc = np.arange(64)[:, None]
    qc = np.arange(64)[None, :]
    cs = np.clip(qc - 8, 0, 48)
    colok = (kc >= cs) & (kc < cs + 16)
    dc = np.clip(kc - qc + 15, 0, 30)
    out = np.full((128, 8, N_NATBL, 128), NEG, dtype=np.float32)
    for sig, t in NA_SIGS.items():
        for qri in range(2):
            for kri in range(2):
                off = sig[qri * 2 + kri]
                if off is None:
                    continue
                for h in range(8):
                    blk = np.where(colok, rpb[h, off + 7][dc], np.float32(NEG))
                    out[kri * 64:(kri + 1) * 64, h, t, qri * 64:(qri + 1) * 64] = blk
    return np.ascontiguousarray(out.reshape(128, 8 * N_NATBL, 128))


def build_sw_mask():
    kk = np.arange(128)[:, None]
    qq = np.arange(128)[None, :]
    m = np.zeros((128, 3, 128), dtype=np.float32)
    m[:, 0, :] = np.where(kk >= qq, 0.0, NEG)
    m[:, 1, :] = np.where(kk <= qq, 0.0, NEG)
    return m


def build_rope():
    half = 32
    inv = (np.float32(10000.0) ** (-(np.arange(half, dtype=np.float32) / np.float32(half)))).astype(np.float32)
    pos = np.arange(SEQ, dtype=np.float32)
    ang = (pos[:, None] * inv[None, :]).astype(np.float32)
    cos = np.cos(ang).astype(np.float32).T
    sin = np.sin(ang).astype(np.float32).T
    C = np.zeros((128, SEQ), np.float32)
    S = np.zeros((128, SEQ), np.float32)
    for p in range(128):
        d = p % 64
        i = d % 32
        C[p] = cos[i]
        S[p] = -sin[i] if d < 32 else sin[i]
    return C, S


def build_perm():
    pm = np.zeros((128, 128), np.float32)
    for m in range(128):
        hb, d = divmod(m, 64)
        pm[hb * 64 + (d + 32) % 64, m] = 1.0
    return pm


class _Stop(Exception):
    pass


def build_program(stop_after=None, dumps=()):
    nc = bass.Bass("TRN2", target_bir_lowering=False)
    P = Prog({"sp": 12, "pool": 10})
    dump_src = {}

    def ckpt(tag):
        if stop_after != tag:
            return
        P.barrier()
        for nm in dumps:
            src = dump_src[nm]
            dt_ = nc.dram_tensor("dbg_" + nm, list(src.shape), src.dtype, kind="ExternalOutput").ap()
            P.add("sp", lambda e, o=dt_, i=src: e.dma_start(out=o, in_=i), [], [], dma=True, name="dump")
        raise _Stop()

    def din(name, shape):
        return nc.dram_tensor(name, list(shape), F32, kind="ExternalInput").ap()

    x_d = din("x", (SEQ_PER_CORE, SEQ, D))
    wada_d = din("w_ada", (D, 6 * D))
    win_d = din("w_in", (D, 2304))
    wout_d = din("w_out", (D, D))
    wup_d = din("w_up", (D, 2 * DFF))
    wdn_d = din("w_down", (DFF, D))
    params_d = din("params", (256, 128))
    gfin_d = din("g_final", (D,))
    sink_d = din("sw_sink", (8,))
    nab_d = din("nabias", (128, 8 * N_NATBL, 128))
    swm_d = din("swmask", (128, 3, 128))
    ropec_d = din("rope_c", (128, SEQ))
    ropes_d = din("rope_s", (128, SEQ))
    ident_d = din("ident", (128, 128))
    perm_d = din("perm", (128, 128))
    out_d = nc.dram_tensor("out", [SEQ_PER_CORE, SEQ, D], F32, kind="ExternalOutput").ap()

    cur = [16512]
    LIMIT = 229344

    def alloc(name, shape, dt, at=None):
        nbytes = int(np.prod(shape[1:])) * (4 if dt == F32 else 2)
        nbytes = (nbytes + 31) // 32 * 32
        if at is None:
            off = cur[0]
            cur[0] += nbytes
        else:
            off = at
        assert off + nbytes <= LIMIT, (name, off, nbytes)
        return nc.alloc_sbuf_tensor_at(name, list(shape), dt, offset=off)

    ident_bf = alloc("ident_bf", [128, 128], BF16)
    ident_f = alloc("ident_f", [128, 128], F32)
    perm_f = alloc("perm_f", [128, 128], F32)
    paramsT = alloc("paramsT", [128, 256], F32)
    modT = alloc("modT", [128, 48, 2], F32)
    modv = alloc("modv", [128, 2, 6, 8], F32)
    esink = alloc("esink", [128, 8], F32)
    swm = alloc("swm", [128, 3, 128], BF16)
    scT = alloc("scT", [128, 2, 8], BF16)
    pstage = [alloc(f"pstage{i}", [128, 128], F32) for i in range(2)]
    tmpf = [alloc(f"tmpf{i}", [128, 128], F32) for i in range(2)]
    NSTAT = 8
    stat = alloc("stat", [128, NSTAT, 4], F32)
    rden_t = alloc("rden_t", [128, 2, 8], F32)
    dtmp_t = alloc("dtmp_t", [128, 2, 8], F32)
    tmp8 = alloc("tmp8", [128, 8], F32)
    tmp8b = alloc("tmp8b", [128, 8], F32)
    epsc = alloc("epsc", [128, 8], F32)
    A0 = cur[0]
    ARENA = LIMIT - A0

    def at(off):
        return A0 + off

    R0, R1, R2, R3 = 0, 32768, 114944, 135488
    hT = alloc("hT", [128, 8, SEQ], BF16, at(R0))
    qna = alloc("qna", [128, 4, SEQ], BF16, at(R1))
    knam = alloc("knam", [128, 8, SEQ], BF16, at(R1 + 16384))
    vna = alloc("vna", [128, NT, 8, 65], BF16, at(R1 + 49152))
    qtsw = alloc("qtsw", [128, 4, SEQ], BF16, at(R1 + 49152 + 16640))
    ktswm = alloc("ktswm", [128, 4, SEQ], BF16, at(R2))
    vsw = alloc("vsw", [128, NT, 2, 65], BF16, at(R2 + 16384))
    win_sb = alloc("win_sb", [128, 8, 2432], BF16, at(R3))
    ropec = alloc("ropec", [128, SEQ], F32, at(R3 + 38912))
    ropes = alloc("ropes", [128, SEQ], F32, at(R3 + 38912 + 8192))
    T3 = R3 + 38912 + 16384
    xin = [alloc(f"xin{i}", [128, 1024], F32, at(T3 + 4096 * i)) for i in range(2)]
    xn = [alloc(f"xn{i}", [128, 1024], BF16, at(T3 + 8192 + 2048 * i)) for i in range(2)]
    qf = [alloc(f"qf{i}", [128, 512], F32, at(T3 + 2048 * i)) for i in range(2)]
    ra = [alloc(f"ra{i}", [128, 512], F32, at(T3 + 4096 + 2048 * i)) for i in range(2)]
    wada_sb = [alloc(f"wada{i}", [128, 8, 512], BF16, at(R3 + 8192 * i)) for i in range(2)]
    wada2_sb = [alloc(f"wada2_{i}", [128, 8, 512], BF16, at(R1 + 49152 + 16640 + 8192 * i)) for i in range(2)]
    nab = alloc("nab", [128, 8 * N_NATBL, 128], BF16, at(R0))
    cat = alloc("cat", [128, NT, 1024], BF16, at(R3))
    ptna = [alloc(f"ptna{i}", [128, 640], BF16, at(R3 + 32768 + 1280 * i)) for i in range(3)]
    ptsw = [alloc(f"ptsw{i}", [128, 3, 512], BF16, at(R3 + 32768 + 3072 * i)) for i in range(2)]
    osb = [alloc(f"osb{i}", [128, 512], F32, at(R3 + 32768 + 6144 + 2048 * i)) for i in range(2)]
    woutg = alloc("woutg", [128, 8, 1024], BF16, at(R3 + 45056))
    wstg = [alloc(f"wstg{i}", [128, 1024], F32, at(R3 + 61440 + 4096 * i)) for i in range(2)]
    cT = [alloc(f"cT{i}", [128, 8, 128], BF16, at(R3 + 32768 + 2048 * i)) for i in range(2)]
    xin5 = [alloc(f"xin5_{i}", [128, 1024], F32, at(R3 + 36864 + 4096 * i)) for i in range(2)]
    xn5 = [alloc(f"xn5_{i}", [128, 1024], BF16, at(R1 + 65536 + 12288 + 2048 * i)) for i in range(2)]
    gatea_bc = alloc("gatea_bc", [128, 1024], F32, at(R0 + 18432))
    gatef_bc = alloc("gatef_bc", [128, 1024], F32, at(R1 + 65536 + 4096))
    gfin_bc = alloc("gfin_bc", [128, 1024], F32, at(R1 + 65536 + 8192))
    x1 = alloc("x1", [128, NT, 1024], F32, at(R1))
    h2T = alloc("h2T", [128, 8, SEQ], BF16, at(R0))
    wdng = alloc("wdng", [128, 6, 1024], BF16, at(R2))
    uT = alloc("uT", [128, 6, SEQ], BF16, at(R3))
    Gb = [alloc(f"Gb{i}", [128, 2052], F32, at(R3 + 24576 + 8224 * i)) for i in range(2)]
    T1 = [alloc(f"T1_{i}", [128, SEQ], F32, at(R3 + 41024 + 8192 * i)) for i in range(2)]
    wup_sb = [alloc(f"wup{i}", [128, 8, 256], BF16, at(R3 + 57408 + 4096 * i)) for i in range(3)]
    wdstg = [alloc(f"wdstg{i}", [128, 1024], F32, at(R2 + 12288 + 4096 * i)) for i in range(2)]
    ostg = [alloc("ostg0", [128, 1024], F32, at(R1 + 65536 + 12288)), alloc("ostg1", [128, 1024], F32, at(R1 + 65536))]

    ps = nc.alloc_psum_tensor("ps", [128, 4096], F32)

    def bank(b, n=1):
        return ps[:, 512 * b:512 * (b + n)]

    def bank_bf(b):
        return ps[:, 512 * b:512 * (b + 1)].bitcast(BF16)

    pb = [Res(f"psum{b}") for b in range(8)]

    class RS:
        pass

    r = RS()
    r.const = Res("const")
    r.params = Res("params")
    r.mod = Res("mod")
    r.gatea = Res("gatea")
    r.gatef = Res("gatef")
    r.stat = [Res(f"stat{i}") for i in range(NSTAT)]
    r.rden = [Res("rden0"), Res("rden1")]
    r.dtmp = [Res("dtmp0"), Res("dtmp1")]
    r.tmp8 = Res("tmp8")
    r.tmp8b = Res("tmp8b")
    r.modv = [[Res(f"modv{b}_{k}") for k in range(6)] for b in range(2)]
    r.hT2 = [[Res(f"hT{t}a"), Res(f"hT{t}b")] for t in range(NT)]
    r.hT = [x for p in r.hT2 for x in p]
    r.qna = [[Res(f"qna{c}_{b}") for b in range(4)] for c in range(4)]
    r.knam = [[Res(f"knam{h}_{b}") for b in range(4)] for h in range(8)]
    r.gfin = Res("gfin")
    r.vna = [Res(f"vna{t}") for t in range(NT)]
    r.qtsw = [[Res(f"qtsw{c}_{b}") for b in range(4)] for c in range(4)]
    r.ktswm = [[Res(f"ktswm{c}_{b}") for b in range(4)] for c in range(4)]
    r.vsw = [Res(f"vsw{t}") for t in range(NT)]
    r.winb = {k: Res("win_" + k) for k in ("qa", "ka", "va", "vb", "qb", "kb0", "kb1", "kb2", "kb3")}
    r.ropeC = Res("ropeC")
    r.ropeS = Res("ropeS")
    r.xin = [Res("xin0"), Res("xin1")]
    r.xn = [Res("xn0"), Res("xn1")]
    r.qf = [Res("qf0"), Res("qf1")]
    r.ra = [Res("ra0"), Res("ra1")]
    r.wada = [Res("wada0"), Res("wada1")]
    r.wada2 = [Res("wada2_0"), Res("wada2_1")]
    r.mod2 = Res("mod2")
    r.pstage = [Res("pstage0"), Res("pstage1")]
    r.tmpf = [Res("tmpf0"), Res("tmpf1")]
    r.nabh = [Res(f"nab{h}") for h in range(8)]
    r.cat = [[Res(f"cat{t}_{h}") for h in range(2)] for t in range(NT)]
    r.ptna = [Res(f"ptna{i}") for i in range(3)]
    r.ptsw = [Res(f"ptsw{i}") for i in range(2)]
    r.osb = [Res("osb0"), Res("osb1")]
    r.woutg = Res("woutg")
    r.wstg = [Res("wstg0"), Res("wstg1")]
    r.cT = [Res("cT0"), Res("cT1")]
    r.x1 = [Res(f"x1_{t}") for t in range(NT)]
    r.h2T2 = [[Res(f"h2T{t}a"), Res(f"h2T{t}b")] for t in range(NT)]
    r.h2T = [x for p in r.h2T2 for x in p]
    r.wdng = [Res(f"wdng{i}") for i in range(6)]
    r.uT = [[Res(f"uT{c}_{b}") for b in range(4)] for c in range(6)]
    r.G = [Res("G0"), Res("G1")]
    r.T1 = [Res("T1_0"), Res("T1_1")]
    r.wup = [Res("wup0"), Res("wup1"), Res("wup2")]
    r.wupg = [Res("wupg0"), Res("wupg1"), Res("wupg2")]
    r.wdstg = [Res("wdstg0"), Res("wdstg1")]
    r.ostg = [Res("ostg0"), Res("ostg1")]

    statc = [0]

    def new_stat():
        i = statc[0] % NSTAT
        statc[0] += 1
        return stat[:, i, :], r.stat[i]

    def dma(q, out, in_, reads=(), writes=(), name=""):
        return P.add(q, lambda e, o=out, i=in_: e.dma_start(out=o, in_=i), reads, writes, dma=True, name=name)

    def act(fn, reads, writes, name=""):
        return P.add("act", fn, reads, writes, name=name)

    def dve(fn, reads, writes, name=""):
        return P.add("dve", fn, reads, writes, name=name)

    def pool(fn, reads, writes, name=""):
        return P.add("pool", fn, reads, writes, name=name)

    def pe(fn, reads, writes, name=""):
        return P.add("pe", fn, reads, writes, name=name)

    def rstd_from_ssq(st, sres, n):
        act(lambda e, st=st: e.activation(out=st[:, 2:3], in_=st[:, 0:1], func=AF.Ln, scale=1.0 / n, bias=epsc[:, 0:1]),
            [sres, r.const], [sres], "rstd_ln")
        act(lambda e, st=st: e.activation(out=st[:, 1:2], in_=st[:, 2:3], func=AF.Exp, scale=-0.5),
            [sres], [sres], "rstd_exp")

    def rms_stats(src_ap, src_res, xnbuf, xnres):
        st, sres = new_stat()

        def f1(e, st=st):
            return e.activation(out=xnbuf[:, :], in_=src_ap, func=AF.Square, accum_out=st[:, 0:1])
        act(f1, [src_res], [sres, xnres], "ssq")
        rstd_from_ssq(st, sres, 1024.0)
        return st, sres

    def rms_apply(src_ap, src_res, st, sres, xnbuf, xnres, pstb, dstT, dst_res, t, Acol, Bcol, mres, evac_dve=False,
                  part="all"):
        if part in ("all", "xn"):
            dve(lambda e, st=st: e.tensor_scalar(xnbuf[:, :], src_ap, st[:, 1:2], None, ALU.mult),
                [src_res, sres], [xnres], "xn")
        if part == "xn":
            return
        pbf = bank_bf(pstb)

        def ft(e):
            ins = None
            for c in range(8):
                ins = e.transpose(pbf[:, c * 128:(c + 1) * 128], xnbuf[:, c * 128:(c + 1) * 128], ident_bf[:, :])
            return ins
        pe(ft, [xnres, r.const], [pb[pstb]], "xnT")

        def fe(e):
            ins = None
            for c in range(8):
                ins = e.activation(out=dstT[:, c, t * 128:(t + 1) * 128], in_=pbf[:, c * 128:(c + 1) * 128],
                                   func=AF.Identity, bias=Bcol(c), scale=Acol(c))
            return ins
        if evac_dve:
            def fe2(e):
                ins = None
                for c in range(8):
                    ins = e.tensor_scalar(dstT[:, c, t * 128:(t + 1) * 128], pbf[:, c * 128:(c + 1) * 128],
                                          Acol(c), Bcol(c), ALU.mult, ALU.add)
                return ins
            dve(fe2, [pb[pstb]] + list(mres), list(dst_res), "hT")
        else:
            act(fe, [pb[pstb]] + list(mres), list(dst_res), "hT")

    dump_src.update(dict(paramsT=paramsT[:, :], modT=modT[:, :, :], modv=modv[:, :, :, :], hT=hT[:, :, :],
                         gatea=gatea_bc[:, :], gatef=gatef_bc[:, :], qna=qna[:, :, :], knam=knam[:, :, :], vna=vna[:, :, :, :],
                         qtsw=qtsw[:, :, :], ktswm=ktswm[:, :, :], vsw=vsw[:, :, :, :], cat=cat[:, :, :],
                         x1=x1[:, :, :], h2T=h2T[:, :, :], esink=esink[:, :], scT=scT[:, :, :]))
    try:
        dve(lambda e: e.memset(epsc[:, :], EPS), [], [r.const], "eps")
        dma("sp", ident_f[:, :], ident_d, [], [r.const])
        dma("sp", perm_f[:, :], perm_d, [], [r.const])
        dma("pool", ident_bf[:, :], ident_d, [], [r.const])
        dma("pool", swm[:, :, :], swm_d, [], [r.const])
        dma("sp", esink[:, :], sink_d.partition_broadcast(128), [], [r.const])
        act(lambda e: e.activation(out=esink[:, :], in_=esink[:, :], func=AF.Exp), [r.const], [r.const], "esink")
        for u in range(2):
            dma("sp", pstage[u][:, :], params_d[u * 128:(u + 1) * 128, :], [], [r.pstage[u]])
            pe(lambda e, u=u: e.transpose(bank(1 + u)[:, 0:128], pstage[u][:, :], ident_f[:, :]),
               [r.pstage[u], r.const], [pb[1 + u]], "paramsT")
            dve(lambda e, u=u: e.tensor_copy(paramsT[:, u * 128:(u + 1) * 128], bank(1 + u)[:, 0:128]),
                [pb[1 + u]], [r.params])
        act(lambda e: e.activation(out=scT[:, :, :].rearrange("p b k -> p (b k)"), in_=paramsT[:, 160:176], func=AF.Silu),
            [r.params], [r.mod], "silu_c")
        wada_v = wada_d.rearrange("(k p) f -> p k f", p=128)
        def mod_tile(ct, bufs, bres, accb):
            wb = ct % 2
            dma("pool", bufs[wb][:, :, :], wada_v[:, :, ct * 512:(ct + 1) * 512], [], [bres[wb]])

            def fm(e):
                ins = None
                for fcl in range(4):
                    fc = ct * 4 + fcl
                    for k in range(8):
                        ins = e.matmul(bank(accb)[:, fc * 2:fc * 2 + 2], bufs[wb][:, k, fcl * 128:(fcl + 1) * 128],
                                       scT[:, :, k], start=(k == 0), stop=(k == 7))
                return ins
            pe(fm, [bres[wb], r.mod], [pb[accb]], "modmm")

        def mod_finish(f0, f1, accb, mres):
            dve(lambda e: e.tensor_tensor(modT[:, f0:f1, :], bank(accb)[:, 2 * f0:2 * f1].rearrange("p (f b) -> p f b", b=2),
                                          paramsT[:, f0:f1].unsqueeze(2).broadcast_to([128, f1 - f0, 2]), ALU.add),
                [pb[accb], r.params], [mres], "modT")

        for ct in range(4):
            mod_tile(ct, wada_sb, r.wada, 0)
        mod_finish(0, 16, 0, r.mod)
        for b in range(2):
            ta = tmp8[:, 0:8]
            dve(lambda e, b=b, ta=ta: e.tensor_scalar(ta, modT[:, 8:16, b], 1.0, None, ALU.add), [r.mod], [r.tmp8], "m1")
            dve(lambda e, b=b, ta=ta: e.tensor_tensor(modv[:, b, 0, :], ta, paramsT[:, 48:56], ALU.mult),
                [r.tmp8, r.params], [r.modv[b][0]], "m2")
            dve(lambda e, b=b: e.tensor_copy(modv[:, b, 1, :], modT[:, 0:8, b]), [r.mod], [r.modv[b][1]], "m3")

        def mod_rest_finish():
            mod_finish(16, 48, 3, r.mod2)
            for b in range(2):
                tb_ = tmp8b[:, 0:8]
                dve(lambda e, b=b: e.tensor_copy(modv[:, b, 2, :], modT[:, 16:24, b]), [r.mod2], [r.modv[b][2]], "m4")
                dve(lambda e, b=b, tb_=tb_: e.tensor_scalar(tb_, modT[:, 32:40, b], 1.0, None, ALU.add), [r.mod2], [r.tmp8b], "m5")
                dve(lambda e, b=b, tb_=tb_: e.tensor_tensor(modv[:, b, 3, :], tb_, paramsT[:, 56:64], ALU.mult),
                    [r.tmp8b, r.params], [r.modv[b][3]], "m6")
                dve(lambda e, b=b: e.tensor_copy(modv[:, b, 4, :], modT[:, 24:32, b]), [r.mod2], [r.modv[b][4]], "m7")
                dve(lambda e, b=b: e.tensor_copy(modv[:, b, 5, :], modT[:, 40:48, b]), [r.mod2], [r.modv[b][5]], "m8")
                gate_to_dram(b, 2)
                gate_to_dram(b, 5)

        def build_gate_bc(b, kind, dst, dres):
            for c in range(8):
                tb = c % 2
                dve(lambda e, c=c, tb=tb: e.tensor_copy(tmpf[tb][:, :], modv[:, b, kind, c:c + 1].to_broadcast([128, 128])),
                    [r.modv[b][kind]], [r.tmpf[tb]], "gbc0")
                pe(lambda e, tb=tb: e.transpose(bank(1 + tb)[:, 0:128], tmpf[tb][:, :], ident_f[:, :]),
                   [r.tmpf[tb], r.const], [pb[1 + tb]], "gbcT")
                dve(lambda e, c=c, tb=tb: e.tensor_copy(dst[:, c * 128:(c + 1) * 128], bank(1 + tb)[:, 0:128]),
                    [pb[1 + tb]], [dres], "gbc1")

        gsc = {}

        def gate_to_dram(b, kind):
            sc = nc.dram_tensor(f"gsc_{b}_{kind}", [1024], F32).ap()
            res = Res(f"gsc{b}_{kind}")
            gsc[(b, kind)] = (sc, res)
            dma("sp", sc.rearrange("(c p) -> p c", p=128), modv[:, b, kind, :], [r.modv[b][kind]], [res])

        def gate_bcast(b, kind, dst, dres):
            sc, res = gsc[(b, kind)]
            dma("sp", dst[:, :], sc.partition_broadcast(128), [res], [dres])

        ckpt("P0")
        out_dmas = []
        for s in range(SEQ_PER_CORE):
            if s > 0:
                P.barrier()
            win_v = win_d.rearrange("(k p) f -> p k f", p=128)
            dma("pool", win_sb[:, :, 0:512], win_v[:, :, 0:512], [], [r.winb["qa"]] + r.wada)
            dma("pool", win_sb[:, :, 512:1024], win_v[:, :, 512:1024], [], [r.winb["ka"]] + r.wada)
            def fones(e):
                e.memset(vna[:, :, :, 64:65], 1.0)
                return e.memset(vsw[:, :, :, 64:65], 1.0)
            pool(fones, [], r.vna + r.vsw, "ones")

            def fzero(e):
                kv = knam[:, :, :].rearrange("p (a two) t -> p a two t", two=2)
                e.memset(kv[64:128, :, 0, :], 0.0)
                e.memset(kv[0:64, :, 1, :], 0.0)
                sv = ktswm[:, :, :].rearrange("p (a two) t -> p a two t", two=2)
                e.memset(sv[64:128, :, 0, :], 0.0)
                return e.memset(sv[0:64, :, 1, :], 0.0)
            pool(fzero, [], [x for l in r.knam for x in l] + [x for l in r.ktswm for x in l], "kzero")
            dma("pool", win_sb[:, :, 1024:1536], win_v[:, :, 1024:1536], [], [r.winb["va"]] + r.wada)
            dma("pool", win_sb[:, :, 2304:2432], win_v[:, :, 2176:2304], [], [r.winb["vb"]] + r.wada)
            dma("pool", win_sb[:, :, 1536:2048], win_v[:, :, 1536:2048], [], [r.winb["qb"]] + r.wada)
            dma("pool", win_sb[:, :, 2048:2112], win_v[:, :, 2048:2112], [], [r.winb["kb0"]] + r.wada)
            dma("pool", win_sb[:, :, 2112:2176], win_v[:, :, 2048:2112], [], [r.winb["kb1"]] + r.wada)
            dma("pool", win_sb[:, :, 2176:2240], win_v[:, :, 2112:2176], [], [r.winb["kb2"]] + r.wada)
            dma("pool", win_sb[:, :, 2240:2304], win_v[:, :, 2112:2176], [], [r.winb["kb3"]] + r.wada)

            p1st = {}

            def p1_load(t):
                dma("sp", xin[t % 2][:, :], x_d[s, t * 128:(t + 1) * 128, :], [], [r.xin[t % 2]])

            def p1_iter(it):
                def ap(t, part):
                    xb = t % 2
                    st_, sres_ = p1st[t]
                    rms_apply(xin[xb][:, :], r.xin[xb], st_, sres_, xn[xb], r.xn[xb], 6 + xb, hT, r.hT2[t], t,
                              lambda c, s=s: modv[:, s, 0, c:c + 1], lambda c, s=s: modv[:, s, 1, c:c + 1], r.modv[s][0:2],
                              evac_dve=True, part=part)
                if it < NT:
                    t, xb = it, it % 2
                    p1st[t] = rms_stats(xin[xb][:, :], r.xin[xb], xn[xb], r.xn[xb])
                if it >= 1:
                    ap(it - 1, "rest")
                if it < NT:
                    ap(it, "xn")
                if it + 1 < NT:
                    p1_load(it + 1)

            fm_plain, fm_rope = [], []
            for cc in range(4):
                fm_plain.append(("qa", qna, cc, r.qna[cc], cc * 128))
            for cc in range(4):
                fm_plain.append(("ka", knam, cc, None, 512 + cc * 128))
            for cc in range(4):
                fm_rope.append(("qb", qtsw, cc, r.qtsw[cc], 1536 + cc * 128))
            for cc in range(2):
                fm_rope.append(("kb", ktswm, cc, None, 2048 + cc * 128))
            pcnt = [0]
            rcnt = [0]
            pending = []
            first_rope = [True]

            def p2_group(kind, dst, dc_, dres, col, b4):
                pbk = pcnt[0] % 3
                pcnt[0] += 1

                def fmm(e):
                    ins = None
                    for k in range(8):
                        ins = e.matmul(bank(pbk), win_sb[:, k, col:col + 128], hT[:, k, b4 * 512:(b4 + 1) * 512],
                                       start=(k == 0), stop=(k == 7))
                    return ins
                wres = [r.winb[kind]] if kind != "kb" else [r.winb["kb%d" % i] for i in range(4)]
                pe(fmm, wres + r.hT[8 * b4:8 * b4 + 8], [pb[pbk]], "proj")
                bsl = slice(b4 * 512, (b4 + 1) * 512)
                dsl = dst[:, dc_, bsl]
                if kind == "qa":
                    act(lambda e: e.activation(out=dsl, in_=bank(pbk), func=AF.Copy, scale=0.125),
                        [pb[pbk]], [dres[b4]], "qa")
                elif kind == "ka":
                    def fka(e):
                        e.activation(out=knam[0:64, 2 * dc_, bsl], in_=bank(pbk)[0:64, :], func=AF.Copy)
                        return e.activation(out=knam[64:128, 2 * dc_ + 1, bsl], in_=bank(pbk)[64:128, :], func=AF.Copy)
                    act(fka, [pb[pbk]], [r.knam[2 * dc_][b4], r.knam[2 * dc_ + 1][b4]], "ka")
                else:
                    rb = rcnt[0] % 2
                    rcnt[0] += 1
                    sc = 0.125 if kind == "qb" else 1.0
                    extra = [r.xin[0], r.xin[1], r.xn[0], r.xn[1]] + r.wada2 if first_rope[0] else []
                    first_rope[0] = False
                    act(lambda e: e.activation(out=qf[rb][:, :], in_=bank(pbk), func=AF.Copy, scale=sc),
                        [pb[pbk]], [r.qf[rb]] + extra, "qf")

                    def post():
                        pe(lambda e: e.matmul(bank(3 + rb), perm_f[:, :], qf[rb][:, :], start=True, stop=True),
                           [r.qf[rb], r.const], [pb[3 + rb]], "perm")
                        pool(lambda e: e.tensor_tensor(ra[rb][:, :], qf[rb][:, :], ropec[:, bsl], ALU.mult),
                             [r.qf[rb], r.ropeC], [r.ra[rb]], "ropeA")
                        dve(lambda e: e.tensor_tensor(qf[rb][:, :], bank(3 + rb), ropes[:, bsl], ALU.mult),
                            [pb[3 + rb], r.ropeS], [r.qf[rb]], "ropeB")
                        if kind == "qb":
                            dve(lambda e: e.tensor_tensor(dsl, qf[rb][:, :], ra[rb][:, :], ALU.add),
                                [r.ra[rb], r.qf[rb]], [dres[b4]], "ropeC")
                        else:
                            def fkb(e):
                                e.tensor_tensor(ktswm[0:64, 2 * dc_, bsl], qf[rb][0:64, :], ra[rb][0:64, :], ALU.add)
                                return e.tensor_tensor(ktswm[64:128, 2 * dc_ + 1, bsl], qf[rb][64:128, :], ra[rb][64:128, :], ALU.add)
                            dve(fkb, [r.ra[rb], r.qf[rb]], [r.ktswm[2 * dc_][b4], r.ktswm[2 * dc_ + 1][b4]], "ropeC")
                    pending.append(post)
                    if len(pending) > 1:
                        pending.pop(0)()

            def p2_v(t):
                hv = (t % 2) * 128

                def fv(e):
                    ins = None
                    for k in range(8):
                        ins = e.matmul(bank(5), hT[:, k, t * 128:(t + 1) * 128], win_sb[:, k, 1024:1536],
                                       start=(k == 0), stop=(k == 7))
                    for k in range(8):
                        ins = e.matmul(bank(4)[:, hv:hv + 128], hT[:, k, t * 128:(t + 1) * 128],
                                       win_sb[:, k, 2304:2432], start=(k == 0), stop=(k == 7))
                    return ins
                pe(fv, [r.winb["va"], r.winb["vb"]] + r.hT2[t], [pb[5], pb[4]], "vproj")
                dve(lambda e: e.tensor_copy(vna[:, t, :, 0:64], bank(5).rearrange("p (h d) -> p h d", d=64)),
                    [pb[5]], [r.vna[t]], "vna")
                dve(lambda e: e.tensor_copy(vsw[:, t, :, 0:64], bank(4)[:, hv:hv + 128].rearrange("p (h d) -> p h d", d=64)),
                    [pb[4]], [r.vsw[t]], "vsw")

            modq = list(range(4, 12)) if s == 0 else []
            p1_load(0)
            for it in range(5):
                p1_iter(it)
            for b4 in range(4):
                if b4 == 1:
                    dma("sp", ropec[:, :], ropec_d, [], [r.ropeC])
                    dma("sp", ropes[:, :], ropes_d, [], [r.ropeS])
                slots = [lambda ch=ch, b4=b4: p2_group(*ch, b4) for ch in fm_plain] + \
                        [lambda t=t: p2_v(t) for t in range(4 * b4, 4 * b4 + 4)]
                nxt = [it for it in range(4 * b4 + 5, 4 * b4 + 9) if it <= NT]
                for si, slot in enumerate(slots):
                    slot()
                    if si % 3 == 2 and nxt:
                        p1_iter(nxt.pop(0))
                    if s == 0 and b4 >= 2 and si % 3 == 1 and modq:
                        mod_tile(modq.pop(0), wada2_sb, r.wada2, 3)
                while nxt:
                    p1_iter(nxt.pop(0))
            while s == 0 and modq:
                mod_tile(modq.pop(0), wada2_sb, r.wada2, 3)
            if s == 0:
                mod_rest_finish()
            ckpt("P1")
            for ch in fm_rope:
                for b4 in range(4):
                    p2_group(*ch, b4)
            while pending:
                pending.pop(0)()

            ckpt("P2")
            P.barrier()
            for h_ in range(8):
                dma("pool", nab[:, h_ * N_NATBL:(h_ + 1) * N_NATBL, :], nab_d[:, h_ * N_NATBL:(h_ + 1) * N_NATBL, :], [], [r.nabh[h_]])
            gate_bcast(s, 2, gatea_bc, r.gatea)

            def issue_wout_fold(c):
                wb = c % 2
                dma("sp", wstg[wb][:, :], wout_d[c * 128:(c + 1) * 128, :], [], [r.wstg[wb]])
                dve(lambda e: e.scalar_tensor_tensor(woutg[:, c, :], wstg[wb][:, :], paramsT[:, 64 + c:65 + c],
                                                     gatea_bc[:, :], ALU.mult, ALU.mult),
                    [r.wstg[wb], r.params, r.gatea], [r.woutg], "woutg")

            def oacc_view(b0):
                return [ps[:, 512 * (b0 + k):512 * (b0 + k + 1)].rearrange("p (h d) -> p h d", d=128) for k in range(2)]

            def attn_finish(ob, tile, half, with_sink, cnt, defer_b=False):
                ov = oacc_view(ob)
                rd = cnt % 2
                rdv = [rden_t[:, rd, 4 * k:4 * k + 4] for k in range(2)]
                if with_sink:
                    dtv = [dtmp_t[:, rd, 4 * k:4 * k + 4] for k in range(2)]

                    def fs(e):
                        ins = None
                        for k in range(2):
                            ins = e.tensor_tensor(dtv[k], ov[k][:, :, 64], esink[:, 4 * k:4 * k + 4], ALU.add)
                        return ins
                    dve(fs, [pb[ob], pb[ob + 1], r.const], [r.dtmp[rd]], "dsink")

                    def fr(e):
                        ins = None
                        for k in range(2):
                            ins = e.reciprocal(rdv[k], dtv[k])
                        return ins
                    dve(fr, [r.dtmp[rd]], [r.rden[rd]], "rden")
                else:
                    def fr(e):
                        ins = None
                        for k in range(2):
                            ins = e.reciprocal(rdv[k], ov[k][:, :, 64])
                        return ins
                    dve(fr, [pb[ob], pb[ob + 1]], [r.rden[rd]], "rden")
                ob_ = osb[rd]

                def fn_(e):
                    ins = None
                    for k in range(2):
                        ins = e.tensor_tensor(ob_[:, 256 * k:256 * (k + 1)].rearrange("p (h d) -> p h d", h=4), ov[k][:, :, 0:64],
                                              rdv[k].unsqueeze(2).broadcast_to([128, 4, 64]), ALU.mult)
                    return ins
                dve(fn_, [pb[ob], pb[ob + 1], r.rden[rd]], [r.osb[rd]], "onorm")

                def part_b():
                    st, sres = new_stat()
                    dsl = cat[:, tile, half * 512:(half + 1) * 512]
                    act(lambda e: e.activation(out=dsl, in_=ob_[:, :], func=AF.Square, accum_out=st[:, 0:1]),
                        [r.osb[rd]], [sres, r.cat[tile][half]], "ossq")
                    rstd_from_ssq(st, sres, 512.0)
                    dve(lambda e: e.tensor_scalar(dsl, ob_[:, :], st[:, 1:2], None, ALU.mult),
                        [r.osb[rd], sres], [r.cat[tile][half]], "ocat")
                if defer_b:
                    return part_b
                part_b()
                return None

            items = [(j, h) for j in range(NT) for h in range(8)]

            def na_qk(idx):
                j, h = items[idx]
                lo, hi = na_tiles(j)
                sb_ = idx % 2
                cc, hp = h // 2, h % 2

                def f(e):
                    ins = None
                    for i in range(lo, hi + 1):
                        o = ps[:, 1024 * sb_ + (i - lo) * 128:1024 * sb_ + (i - lo + 1) * 128]
                        e.matmul(o, knam[:, h, i * 128:(i + 1) * 128], qna[:, cc, j * 128:(j + 1) * 128],
                                 start=True, stop=False)
                        ins = e.matmul(o, ident_bf[:, :], nab[:, h * N_NATBL + NA_TBL_OF[(j, i)], :], start=False, stop=True)
                    return ins
                rr = [r.qna[cc][j // 4], r.nabh[h], r.const] + [r.knam[h][i // 4] for i in range(lo, hi + 1)]
                pe(f, rr, [pb[2 * sb_], pb[2 * sb_ + 1]], "naqk")

            def na_exp(idx):
                j, h = items[idx]
                lo, hi = na_tiles(j)
                n = hi - lo + 1
                sb_ = idx % 2
                pt = idx % 3
                def fx(e):
                    n1 = min(n, 4)
                    ins = e.activation(out=ptna[pt][:, 0:n1 * 128], in_=ps[:, 1024 * sb_:1024 * sb_ + n1 * 128], func=AF.Exp)
                    if n > 4:
                        ins = e.activation(out=ptna[pt][:, 512:640], in_=ps[:, 1024 * sb_ + 512:1024 * sb_ + 640], func=AF.Exp)
                    return ins
                act(fx, [pb[2 * sb_], pb[2 * sb_ + 1]], [r.ptna[pt]], "naexp")

            def na_pv(idx):
                j, h = items[idx]
                lo, hi = na_tiles(j)
                pt = idx % 3
                ob = 4 + 2 * (j % 2)
                ov = oacc_view(ob)

                def f(e):
                    ins = None
                    for i in range(lo, hi + 1):
                        ins = e.matmul(ov[h // 4][:, h % 4, 0:65], ptna[pt][:, (i - lo) * 128:(i - lo + 1) * 128],
                                       vna[:, i, h, :], start=(i == lo), stop=(i == hi))
                    return ins
                pe(f, [r.ptna[pt]] + [r.vna[i] for i in range(lo, hi + 1)], [pb[ob + h // 4]], "napv")
                if h == 7:
                    fin_q.append((idx + 2, lambda: attn_finish(ob, j, 0, False, j)))

            fin_q = []
            na_qk(0)
            for idx in range(len(items)):
                na_exp(idx)
                while fin_q and fin_q[0][0] <= idx:
                    fin_q.pop(0)[1]()
                if idx + 1 < len(items):
                    na_qk(idx + 1)
                na_pv(idx)
                if idx % 8 == 3 and idx // 8 < 8:
                    issue_wout_fold(idx // 8)
            while fin_q:
                fin_q.pop(0)[1]()

            ckpt("P3")
            P.barrier()
            sitems = [(n, g) for n in range(NT) for g in range(2)]

            def sw_deltas(n):
                return [d for d in (-1, 0, 1) if 0 <= n + d < NT]

            def sw_qk(idx):
                n, g = sitems[idx]
                sb_ = idx % 2
                base = 1536 * sb_

                def f(e):
                    ins = None
                    for d in sw_deltas(n):
                        di = d + 1
                        kb = n + d
                        for hh in range(4):
                            hp = hh % 2
                            o = ps[:, base + di * 512 + hh * 128:base + di * 512 + (hh + 1) * 128]
                            ins = e.matmul(o, ktswm[:, 2 * g + hp, kb * 128:(kb + 1) * 128],
                                           qtsw[:, 2 * g + hh // 2, n * 128:(n + 1) * 128], start=True, stop=(d == 0))
                            if d != 0:
                                mi = 0 if d == -1 else 1
                                ins = e.matmul(o, ident_bf[:, :], swm[:, mi, :], start=False, stop=True)
                    return ins
                rr = [r.const, r.qtsw[2 * g][n // 4], r.qtsw[2 * g + 1][n // 4]] + [r.ktswm[2 * g + hp_][(n + d) // 4] for d in sw_deltas(n) for hp_ in range(2)]
                pe(f, rr, [pb[3 * sb_], pb[3 * sb_ + 1], pb[3 * sb_ + 2]], "swqk")

            def sw_exp(idx):
                n, g = sitems[idx]
                sb_ = idx % 2
                base = 1536 * sb_
                ds_ = sw_deltas(n)
                d0, d1 = ds_[0] + 1, ds_[-1] + 1
                def fx(e):
                    ins = None
                    for di in range(d0, d1 + 1):
                        ins = e.activation(out=ptsw[sb_][:, di, :], in_=ps[:, base + di * 512:base + (di + 1) * 512], func=AF.Exp)
                    return ins
                act(fx, [pb[3 * sb_], pb[3 * sb_ + 1], pb[3 * sb_ + 2]], [r.ptsw[sb_]], "swexp")

            def sw_pv(idx):
                n, g = sitems[idx]
                sb_ = idx % 2
                ov = oacc_view(6)
                ds_ = sw_deltas(n)

                def f(e):
                    ins = None
                    for hh in range(4):
                        for d in ds_:
                            ins = e.matmul(ov[g][:, hh, 0:65], ptsw[sb_][:, d + 1, hh * 128:(hh + 1) * 128],
                                           vsw[:, n + d, g, :], start=(d == ds_[0]), stop=(d == ds_[-1]))
                    return ins
                pe(f, [r.ptsw[sb_]] + [r.vsw[n + d] for d in ds_], [pb[6 + g]], "swpv")
                if g == 1:
                    fin_sw.append((idx + 2, attn_finish(6, n, 1, True, n, defer_b=True)))

            fin_sw = []
            sw_qk(0)
            for idx in range(len(sitems)):
                sw_exp(idx)
                while fin_sw and fin_sw[0][0] <= idx:
                    fin_sw.pop(0)[1]()
                if idx + 1 < len(sitems):
                    sw_qk(idx + 1)
                sw_pv(idx)
            while fin_sw:
                fin_sw.pop(0)[1]()

            ckpt("P4")
            P.barrier()
            gate_bcast(s, 5, gatef_bc, r.gatef)
            p5st = {}

            def p5_s1(t):
                tb = t % 2
                pbf = bank_bf(tb)
                dma("sp", xin5[tb][:, :], x_d[s, t * 128:(t + 1) * 128, :], [], [r.xin[tb]])

                def ft(e):
                    ins = None
                    for c in range(8):
                        ins = e.transpose(pbf[:, c * 128:(c + 1) * 128], cat[:, t, c * 128:(c + 1) * 128], ident_bf[:, :])
                    return ins
                pe(ft, [r.cat[t][0], r.cat[t][1], r.const], [pb[tb]], "catT")
                act(lambda e: e.activation(out=cT[tb][:, :, :].rearrange("p c q -> p (c q)"), in_=pbf, func=AF.Copy),
                    [pb[tb]], [r.cT[tb]], "cT")

            def p5_s2(t):
                tb = t % 2
                mb = 2 + 2 * tb

                def fo(e):
                    ins = None
                    for hf in range(2):
                        for c in range(8):
                            ins = e.matmul(bank(mb + hf), cT[tb][:, c, :], woutg[:, c, hf * 512:(hf + 1) * 512],
                                           start=(c == 0), stop=(c == 7))
                    return ins
                pe(fo, [r.cT[tb], r.woutg], [pb[mb], pb[mb + 1]], "outproj")

                def fx1(e):
                    ins = None
                    for hf in range(2):
                        ins = e.tensor_tensor(x1[:, t, hf * 512:(hf + 1) * 512], xin5[tb][:, hf * 512:(hf + 1) * 512],
                                              bank(mb + hf), ALU.add)
                    return ins
                dve(fx1, [r.xin[tb], pb[mb], pb[mb + 1]], [r.x1[t]], "x1")
                p5st[t] = rms_stats(x1[:, t, :], r.x1[t], xn5[tb], r.xn[tb])

            def p5_s3(t, part, s=s):
                tb = t % 2
                st_, sres_ = p5st[t]
                rms_apply(x1[:, t, :], r.x1[t], st_, sres_, xn5[tb], r.xn[tb], 6 + tb, h2T, r.h2T2[t], t,
                          lambda c: modv[:, s, 3, c:c + 1], lambda c: modv[:, s, 4, c:c + 1], r.modv[s][3:5],
                          evac_dve=True, part=part)

            for it in range(NT + 2):
                if 0 <= it - 2 < NT:
                    p5_s3(it - 2, "xn")
                if it < NT:
                    p5_s1(it)
                if 0 <= it - 1 < NT:
                    p5_s2(it - 1)
                if 0 <= it - 2 < NT:
                    p5_s3(it - 2, "rest")

            ckpt("P5")
            P.barrier()
            dma("sp", gfin_bc[:, :], gfin_d.partition_broadcast(128), [], [r.gfin])
            if True:
                def fz(e):
                    for g_ in Gb:
                        e.memset(g_[:, 0:1], 0.0)
                        ins = e.memset(g_[:, 2049:2050], 0.0)
                    return ins
                pool(fz, [], r.G, "gzero")
            wup_v = wup_d.rearrange("(k p) f -> p k f", p=128)
            dcnt = [0]
            group_of = {}
            for gi_, (c0_, c1_) in enumerate(GROUPS):
                for c_ in range(c0_, c1_):
                    group_of[c_] = gi_

            def wup_load(c):
                wb = c % 3
                dma("pool", wup_sb[wb][:, :, 128:256], wup_v[:, :, DFF + c * 128:DFF + (c + 1) * 128], [], [r.wupg[wb]])
                dma("pool", wup_sb[wb][:, :, 0:128], wup_v[:, :, c * 128:(c + 1) * 128], [], [r.wup[wb]])

            def issue_gate(c):
                wb, gb = c % 3, c % 2
                for b4 in range(4):
                    gk = b4 % 2

                    def fg(e, wb=wb, b4=b4, gk=gk):
                        ins = None
                        for k in range(8):
                            ins = e.matmul(bank(gk), wup_sb[wb][:, k, 128:256], h2T[:, k, b4 * 512:(b4 + 1) * 512],
                                           start=(k == 0), stop=(k == 7))
                        return ins
                    pe(fg, [r.wupg[wb]] + r.h2T[8 * b4:8 * b4 + 8], [pb[gk]], "gate")
                    act(lambda e, gb=gb, b4=b4, gk=gk: e.activation(out=Gb[gb][:, 1 + b4 * 512:1 + (b4 + 1) * 512],
                                                                   in_=bank(gk), func=AF.Copy),
                        [pb[gk]], [r.G[gb]], "gevac")

            def issue_conv(c):
                gb = c % 2
                t1, t1r = T1[c % 2], r.T1[c % 2]
                cw0 = paramsT[:, 94 + c:95 + c]
                cw1 = paramsT[:, 94 + 22 + c:95 + 22 + c]
                cw2 = paramsT[:, 94 + 44 + c:95 + 44 + c]
                cbb = paramsT[:, 72 + c:73 + c]
                pool(lambda e: e.tensor_scalar(t1[:, :], Gb[gb][:, 0:2048], cw0, cbb, ALU.mult, ALU.add),
                     [r.G[gb], r.params], [t1r], "conv0")
                dve(lambda e: e.scalar_tensor_tensor(t1[:, :], Gb[gb][:, 1:2049], cw1, t1[:, :], ALU.mult, ALU.add),
                    [r.G[gb], r.params, t1r], [t1r], "conv1")
                dve(lambda e: e.scalar_tensor_tensor(t1[:, :], Gb[gb][:, 2:2050], cw2, t1[:, :], ALU.mult, ALU.add),
                    [r.G[gb], r.params, t1r], [t1r], "conv2")

            def issue_silu(c):
                t1, t1r = T1[c % 2], r.T1[c % 2]
                act(lambda e: e.activation(out=t1[:, :], in_=t1[:, :], func=AF.Silu), [t1r], [t1r], "silu")

            def issue_val(c):
                wb = c % 3
                t1, t1r = T1[c % 2], r.T1[c % 2]
                ci = c - GROUPS[group_of[c]][0]
                for b4 in range(4):
                    vk = 2 + b4 % 2

                    def fvv(e, wb=wb, b4=b4, vk=vk):
                        ins = None
                        for k in range(8):
                            ins = e.matmul(bank(vk), wup_sb[wb][:, k, 0:128], h2T[:, k, b4 * 512:(b4 + 1) * 512],
                                           start=(k == 0), stop=(k == 7))
                        return ins
                    pe(fvv, [r.wup[wb]] + r.h2T[8 * b4:8 * b4 + 8], [pb[vk]], "val")
                    dve(lambda e, ci=ci, b4=b4, vk=vk: e.tensor_tensor(uT[:, ci, b4 * 512:(b4 + 1) * 512],
                                                                      t1[:, b4 * 512:(b4 + 1) * 512], bank(vk), ALU.mult),
                        [t1r, pb[vk]], [r.uT[ci][b4]], "umul")

            def issue_fold(c):
                ci = c - GROUPS[group_of[c]][0]
                wb = c % 2
                dma("sp", wdstg[wb][:, :], wdn_d[c * 128:(c + 1) * 128, :], [], [r.wdstg[wb]])
                dve(lambda e: e.tensor_tensor(wdng[:, ci, :], wdstg[wb][:, :], gatef_bc[:, :], ALU.mult),
                    [r.wdstg[wb], r.gatef], [r.wdng[ci]], "wdng")

            fin6 = []

            def issue_down(gi):
                c0, c1 = GROUPS[gi]
                last = gi == len(GROUPS) - 1
                ng = c1 - c0
                for t in range(NT):
                    for hf in range(2):
                        dk = 4 + dcnt[0] % 4
                        dcnt[0] += 1

                        def fd(e, t=t, hf=hf, dk=dk, ng=ng):
                            ins = None
                            for ci in range(ng):
                                ins = e.matmul(bank(dk), uT[:, ci, t * 128:(t + 1) * 128], wdng[:, ci, hf * 512:(hf + 1) * 512],
                                               start=(ci == 0), stop=(ci == ng - 1))
                            return ins
                        pe(fd, [r.uT[ci][t // 4] for ci in range(ng)] + r.wdng[0:ng], [pb[dk]], "down")
                        dve(lambda e, t=t, hf=hf, dk=dk: e.tensor_tensor(x1[:, t, hf * 512:(hf + 1) * 512],
                                                                        x1[:, t, hf * 512:(hf + 1) * 512], bank(dk), ALU.add),
                            [pb[dk], r.x1[t]], [r.x1[t]], "x2acc")
                    if last:
                        def fstats(t=t):
                            ob = t % 2
                            st, sres = new_stat()
                            act(lambda e: e.activation(out=ostg[ob][:, :], in_=x1[:, t, :], func=AF.Square, accum_out=st[:, 0:1]),
                                [r.x1[t]], [sres, r.ostg[ob]], "fssq")
                            rstd_from_ssq(st, sres, 1024.0)
                            return st, sres

                        def ffinal(stp, t=t):
                            ob = t % 2
                            st, sres = stp
                            dve(lambda e: e.scalar_tensor_tensor(ostg[ob][:, :], x1[:, t, :], st[:, 1:2],
                                                                 gfin_bc[:, :], ALU.mult, ALU.mult),
                                [r.x1[t], sres, r.gfin], [r.ostg[ob]], "final")
                            out_dmas.append(dma("sp", out_d[s, t * 128:(t + 1) * 128, :], ostg[ob][:, :], [r.ostg[ob]], []))
                        fin6.append([t, fstats, ffinal, None])
                        for ent in fin6:
                            if ent[3] is None and ent[0] <= t - 1:
                                ent[3] = ent[1]()
                        while fin6 and fin6[0][0] <= t - 2:
                            ent = fin6.pop(0)
                            ent[2](ent[3])
                if last:
                    for ent in fin6:
                        if ent[3] is None:
                            ent[3] = ent[1]()
                    while fin6:
                        ent = fin6.pop(0)
                        ent[2](ent[3])

            wup_load(0)
            wup_load(1)
            wup_load(2)
            for c in range(NCH):
                if c >= 2 and c + 1 < NCH:
                    wup_load(c + 1)
                issue_gate(c)
                if c > 0:
                    issue_silu(c - 1)
                    issue_val(c - 1)
                issue_conv(c)
                if c > 0 and group_of[c - 1] != group_of[c]:
                    issue_down(group_of[c - 1])
                issue_fold(c)
            issue_silu(NCH - 1)
            issue_val(NCH - 1)
            issue_down(len(GROUPS) - 1)

    except _Stop:
        pass

    P.barrier()

    with ExitStack() as es:
        es.enter_context(nc.allow_low_precision("bf16 matmul operands, fp32 accumulation"))
        es.enter_context(nc.allow_non_contiguous_dma("small strided parameter loads"))
        eng_sems = {e: es.enter_context(nc.semaphore(f"sem_{e}")) for e in ENGS}
        dma_sems = {q: [es.enter_context(nc.semaphore(f"dsem_{q}{i}")) for i in range(n)]
                    for q, n in P.n_dma_sems.items()}
        block = es.enter_context(nc.Block())
        P.emit(nc, block, eng_sems, dma_sems)
    return nc


_CONSTS = {}


def _consts():
    if not _CONSTS:
        C, S = build_rope()
        _CONSTS.update(dict(rope_c=C, rope_s=S, swmask=build_sw_mask(),
                            ident=np.eye(128, dtype=np.float32), perm=build_perm()))
    return _CONSTS


def make_in_maps(x, c, w_ada, b_ada, g_attn, w_in, na_rpb, sw_sink, g_na_out, g_sw_out,
                 w_out, g_ffn, w_up, conv_w, conv_b, w_down, g_final, cores=range(N_CORES)):
    f = lambda a: np.ascontiguousarray(np.asarray(a, dtype=np.float32))
    x = f(x); c = f(c)
    cst = _consts()
    nabias = build_na_bias(f(na_rpb)[0])
    base_rows = [f(b_ada)[0].reshape(48, 128), f(g_attn)[0].reshape(8, 128), f(g_ffn)[0].reshape(8, 128),
                 f(g_na_out)[0].reshape(4, 128), f(g_sw_out)[0].reshape(4, 128),
                 f(conv_b)[0].reshape(22, 128), f(conv_w)[0].reshape(66, 128)]
    shared = dict(w_ada=f(w_ada)[0], w_in=f(w_in)[0], w_out=f(w_out)[0], w_up=f(w_up)[0], w_down=f(w_down)[0],
                  g_final=f(g_final), sw_sink=f(sw_sink)[0], nabias=nabias, **cst)
    in_maps = []
    for core in cores:
        b0 = core * SEQ_PER_CORE
        rows = base_rows + [c[b0].reshape(8, 128), c[b0 + 1].reshape(8, 128)]
        params = np.zeros((256, 128), np.float32)
        cat_rows = np.concatenate(rows, axis=0)
        params[:cat_rows.shape[0]] = cat_rows
        m = dict(shared)
        m["x"] = np.ascontiguousarray(x[b0:b0 + SEQ_PER_CORE])
        m["params"] = params
        in_maps.append(m)
    return in_maps


def kernel(x, c, w_ada, b_ada, g_attn, w_in, na_rpb, sw_sink, g_na_out, g_sw_out,
           w_out, g_ffn, w_up, conv_w, conv_b, w_down, g_final):
    in_maps = make_in_maps(x, c, w_ada, b_ada, g_attn, w_in, na_rpb, sw_sink, g_na_out, g_sw_out,
                           w_out, g_ffn, w_up, conv_w, conv_b, w_down, g_final)
    nc = build_program()
    res = run_bass_kernel_spmd(nc, in_maps, core_ids=list(range(N_CORES)))
    out = np.concatenate([np.asarray(r["out"]) for r in res.results], axis=0)
    return out.astype(np.float32)
```
